# Optimizing a Trainium2 kernel written in Bass

```python
import math
import jax, jax.numpy as jnp
from jax import lax
import numpy as np

D_MODEL = 2048
BATCH = 4
SEQ = 2048
DEPTH = 1
DEC_BATCH = 128
DEC_SEQ = 4
PAST_LEN = 16384
PAGE_SIZE = 128

PLE_DIM = 256
HG_HEADS = 8
HG_K = 128
HG_V = 128
HG_WIDTH = HG_HEADS * HG_V
GLA_HEADS = 4
GLA_K = 128
GLA_V = 256
GLA_KW = GLA_HEADS * GLA_K
GLA_VW = GLA_HEADS * GLA_V
GLA_GATE_RANK = 16
GLA_GATE_NORM = 16.0
BRANCH_W = 1024
N_BRANCH = 2
D_FF = -(-8 * D_MODEL // (3 * 256)) * 256
CHUNK = 32
EPS = 1e-6
IN_SIZES = (HG_HEADS * HG_K, HG_HEADS * HG_K, HG_WIDTH, HG_WIDTH,
            GLA_KW, GLA_KW, GLA_VW, GLA_VW, GLA_GATE_RANK,
            N_BRANCH * D_MODEL)
IN_COLS = sum(IN_SIZES)

kernel_name = "hgrn2_gla_parallel_decoder_step"


def rmsnorm(x, g):
    xf = x.astype(jnp.float32)
    y = xf * lax.rsqrt(jnp.mean(xf * xf, axis=-1, keepdims=True) + EPS)
    return (y * g.astype(jnp.float32)).astype(x.dtype)


def rms_head(o, g):
    return o * lax.rsqrt(jnp.mean(o * o, axis=-1, keepdims=True) + EPS) * g.astype(jnp.float32)


def gated_linear_scan(q, k, v, logf, s0):
    b_, t_, h_, _ = q.shape
    dv = v.shape[-1]
    c = math.gcd(t_, CHUNK)
    n = t_ // c
    rs = lambda a: a.reshape(b_, n, c, h_, a.shape[-1])
    q, k, v, logf = rs(q), rs(k), rs(v), rs(logf)
    cum = jnp.cumsum(logf, axis=2)
    last = cum[:, :, -1:]
    ref = cum[:, :, c // 2:c // 2 + 1]
    qe = q * jnp.exp(cum - ref)
    ke = k * jnp.exp(ref - cum)
    att = jnp.einsum('bnthk,bnshk->bnhts', qe, ke)
    mask = jnp.tril(jnp.ones((c, c), dtype=bool))
    att = jnp.where(mask, att, 0.0)
    o_intra = jnp.einsum('bnhts,bnshv->bnthv', att, v)
    q_in = q * jnp.exp(cum)
    k_out = k * jnp.exp(last - cum)
    decay = jnp.exp(last[:, :, 0])

    def step(s, inp):
        qi, ki, vi, di = inp
        o = jnp.einsum('bthk,bhkv->bthv', qi, s)
        s = s * di[..., None] + jnp.einsum('bthk,bthv->bhkv', ki, vi)
        return s, o

    xs = (jnp.moveaxis(q_in, 1, 0), jnp.moveaxis(k_out, 1, 0),
          jnp.moveaxis(v, 1, 0), jnp.moveaxis(decay, 1, 0))
    s_fin, o_inter = lax.scan(step, s0, xs)
    o = o_intra + jnp.moveaxis(o_inter, 0, 1)
    return o.reshape(b_, t_, h_, dv), s_fin


def hgrn2_branch(q_raw, f_raw, i_raw, g_raw, lb, gain, s0):
    bsz, t_, _ = q_raw.shape
    f32 = jnp.float32
    q = jax.nn.silu(q_raw.astype(f32).reshape(bsz, t_, HG_HEADS, HG_K)) * (HG_K ** -0.5)
    fr = f_raw.astype(f32).reshape(bsz, t_, HG_HEADS, HG_K)
    lbh = lb.reshape(HG_HEADS, HG_K)
    logf = jnp.log(lbh + (1.0 - lbh) * jax.nn.sigmoid(fr))
    k = (1.0 - lbh) * jax.nn.sigmoid(-fr)
    v = i_raw.astype(f32).reshape(bsz, t_, HG_HEADS, HG_V)
    o, s = gated_linear_scan(q, k, v, logf, s0.astype(f32))
    g = g_raw.astype(f32).reshape(bsz, t_, HG_HEADS, HG_V)
    o = rms_head(o, gain) * jax.nn.silu(g)
    return o.reshape(bsz, t_, HG_WIDTH), s


def gla_branch(q_raw, k_raw, v_raw, g_raw, gk_lr, w_gk, b_gk, gain, s0):
    bsz, t_, _ = q_raw.shape
    f32 = jnp.float32
    q = q_raw.astype(f32).reshape(bsz, t_, GLA_HEADS, GLA_K) * (GLA_K ** -0.5)
    k = k_raw.astype(f32).reshape(bsz, t_, GLA_HEADS, GLA_K)
    v = v_raw.astype(f32).reshape(bsz, t_, GLA_HEADS, GLA_V)
    gk = gk_lr.astype(f32) @ w_gk.astype(f32) + b_gk.astype(f32)
    logf = (jax.nn.log_sigmoid(gk) / GLA_GATE_NORM).reshape(bsz, t_, GLA_HEADS, GLA_K)
    o, s = gated_linear_scan(q, k, v, logf, s0.astype(f32))
    g = g_raw.astype(f32).reshape(bsz, t_, GLA_HEADS, GLA_V)
    o = rms_head(o, gain) * jax.nn.silu(g)
    return o.reshape(bsz, t_, GLA_VW), s


def decoder_layer(x, p, s_hg, s_gla, lb, ln1, w_in, hg_norm, gla_w_gk, gla_b_gk, gla_norm,
                  w_branch, w_out, ln2, w_gu, w_down, ln3, w_ple, w_pg):
    bsz, t_, _ = x.shape
    h = rmsnorm(x, ln1)
    z = h @ w_in
    splits = [int(c) for c in np.cumsum(IN_SIZES)[:-1]]
    hq, hf, hi, hgt, gq, gk, gv, gg, glr, mg = jnp.split(z, splits, axis=-1)
    a, s_hg = hgrn2_branch(hq, hf, hi, hgt, lb, hg_norm, s_hg)
    b, s_gla = gla_branch(gq, gk, gv, gg, glr, gla_w_gk, gla_b_gk, gla_norm, s_gla)
    br = jnp.stack([a, b], axis=0).astype(x.dtype)
    up = jnp.einsum('nbtw,nwd->nbtd', br, w_branch)
    gates = jax.nn.sigmoid(mg.astype(jnp.float32).reshape(bsz, t_, N_BRANCH, D_MODEL))
    merged = jnp.einsum('btnd,nbtd->btd', gates, up.astype(jnp.float32)).astype(x.dtype)
    x = x + merged @ w_out
    h2 = rmsnorm(x, ln2)
    gate, upv = jnp.split(h2 @ w_gu, 2, axis=-1)
    x = x + (jax.nn.silu(gate) * upv) @ w_down
    h3 = rmsnorm(x, ln3)
    x = x + jax.nn.sigmoid(h3 @ w_pg) * (p.astype(x.dtype) @ w_ple)
    return x, s_hg, s_gla


def setup_inputs(seed: int = 0) -> dict:
    key = jax.random.key(seed)
    ks = jax.random.split(key, 24)
    f32 = jnp.float32
    nrm = lambda k, shape, s: jax.random.normal(k, shape, f32) * s
    gain = lambda k, shape: 1.0 + 0.02 * jax.random.normal(k, shape, f32)
    return {
        "x_prompt": nrm(ks[0], (BATCH, SEQ, D_MODEL), 1.0),
        "x_sample": nrm(ks[1], (DEC_BATCH, DEC_SEQ, D_MODEL), 1.0),
        "state_hgrn": nrm(ks[2], (DEPTH, DEC_BATCH, HG_HEADS, HG_K, HG_V), 0.5),
        "state_gla": nrm(ks[3], (DEPTH, DEC_BATCH, GLA_HEADS, GLA_K, GLA_V), 0.5),
        "p_prompt": nrm(ks[4], (DEPTH, BATCH, SEQ, PLE_DIM), 1.0),
        "p_sample": nrm(ks[5], (DEPTH, DEC_BATCH, DEC_SEQ, PLE_DIM), 1.0),
        "hg_lb": nrm(ks[6], (DEPTH + 1, HG_HEADS * HG_K), 0.1),
        "ln1": gain(ks[7], (DEPTH, D_MODEL)),
        "w_in": nrm(ks[8], (DEPTH, D_MODEL, IN_COLS), D_MODEL ** -0.5),
        "hg_norm": gain(ks[9], (DEPTH, HG_V)),
        "gla_w_gk": nrm(ks[10], (DEPTH, GLA_GATE_RANK, GLA_KW), GLA_GATE_RANK ** -0.5),
        "gla_b_gk": nrm(ks[11], (DEPTH, GLA_KW), 0.1),
        "gla_norm": gain(ks[12], (DEPTH, GLA_V)),
        "w_branch": nrm(ks[13], (DEPTH, N_BRANCH, BRANCH_W, D_MODEL), BRANCH_W ** -0.5),
        "w_out": nrm(ks[14], (DEPTH, D_MODEL, D_MODEL), D_MODEL ** -0.5),
        "ln2": gain(ks[15], (DEPTH, D_MODEL)),
        "w_gu": nrm(ks[16], (DEPTH, D_MODEL, 2 * D_FF), D_MODEL ** -0.5),
        "w_down": nrm(ks[17], (DEPTH, D_FF, D_MODEL), D_FF ** -0.5),
        "ln3": gain(ks[18], (DEPTH, D_MODEL)),
        "w_ple": nrm(ks[19], (DEPTH, PLE_DIM, D_MODEL), PLE_DIM ** -0.5),
        "w_pg": nrm(ks[20], (DEPTH, D_MODEL, D_MODEL), D_MODEL ** -0.5),
        "ln_f": gain(ks[21], (D_MODEL,)),
    }


def reference(x_prompt, x_sample, state_hgrn, state_gla, p_prompt, p_sample, hg_lb, ln1, w_in,
              hg_norm, gla_w_gk, gla_b_gk, gla_norm, w_branch, w_out, ln2, w_gu, w_down, ln3,
              w_ple, w_pg, ln_f):
    lb_all = jnp.cumsum(jax.nn.softmax(hg_lb.astype(jnp.float32), axis=0), axis=0)
    xp, xs = x_prompt, x_sample
    hp_list, gp_list, hs_list, gs_list = [], [], [], []
    for i in range(DEPTH):
        w = (lb_all[i], ln1[i], w_in[i], hg_norm[i], gla_w_gk[i], gla_b_gk[i], gla_norm[i],
             w_branch[i], w_out[i], ln2[i], w_gu[i], w_down[i], ln3[i], w_ple[i], w_pg[i])
        s_hg0 = jnp.zeros((BATCH, HG_HEADS, HG_K, HG_V), jnp.float32)
        s_gl0 = jnp.zeros((BATCH, GLA_HEADS, GLA_K, GLA_V), jnp.float32)
        xp, hp, gp = decoder_layer(xp, p_prompt[i], s_hg0, s_gl0, *w)
        xs, hs, gs = decoder_layer(xs, p_sample[i], state_hgrn[i], state_gla[i], *w)
        hp_list.append(hp.astype(state_hgrn.dtype))
        gp_list.append(gp.astype(state_gla.dtype))
        hs_list.append(hs.astype(state_hgrn.dtype))
        gs_list.append(gs.astype(state_gla.dtype))
    y_prompt = rmsnorm(xp, ln_f)
    y_sample = rmsnorm(xs, ln_f)
    new_hgrn_prompt = jnp.stack(hp_list, axis=0)
    new_gla_prompt = jnp.stack(gp_list, axis=0)
    new_hgrn_sample = jnp.stack(hs_list, axis=0)
    new_gla_sample = jnp.stack(gs_list, axis=0)
    return (y_prompt, y_sample, new_hgrn_prompt, new_gla_prompt, new_hgrn_sample, new_gla_sample)
```

```python
import contextlib
import numpy as np
import concourse.bass as bass
import concourse.mybir as mybir
from concourse.bass_utils import run_bass_kernel_spmd

F32 = mybir.dt.float32
BF16 = mybir.dt.bfloat16
AF = mybir.ActivationFunctionType
ALU = mybir.AluOpType

D = 2048
NTP = 1024
NSQ = 16
NTS = 64
NT = NTP + NTS
KC = 16
IN_COLS = 11280
DFF = 5632
EPS = 1e-6
C_HQ, C_HF, C_HI, C_HG = 0, 1024, 2048, 3072
C_GQ, C_GK, C_GV, C_GG, C_GLR, C_MG = 4096, 4608, 5120, 6144, 7168, 7184

ENG_NAMES = ["pe", "act", "dve", "pool", "sp"]
SEM_LIMIT = 30000


class Res:
    __slots__ = ("name", "w", "r")

    def __init__(self, name):
        self.name = name
        self.w = {}
        self.r = {}


class Prog:
    def __init__(self, same_engine_sync=("act", "dve", "pool")):
        self.streams = {e: [] for e in ENG_NAMES}
        self.nsem = 0
        self.cur_sem = {}
        self.cur_cnt = {}
        for e in ENG_NAMES:
            self.cur_sem[e] = self._new_sem()
            self.cur_cnt[e] = 0
        self.seen = {e: {} for e in ENG_NAMES}
        self.dma_sems = {}
        self.same_sync = set(same_engine_sync)
        self.all_dma_tokens = {}

    def _new_sem(self):
        self.nsem += 1
        return self.nsem - 1

    def _collect(self, eng, reads, writes):
        deps = {}
        for r in reads:
            for s, c in r.w.items():
                if deps.get(s, 0) < c:
                    deps[s] = c
        for w in writes:
            for s, c in w.w.items():
                if deps.get(s, 0) < c:
                    deps[s] = c
            for s, c in w.r.items():
                if deps.get(s, 0) < c:
                    deps[s] = c
        waits = []
        seen = self.seen[eng]
        for s, c in deps.items():
            if seen.get(s, 0) < c:
                waits.append((s, c))
                seen[s] = c
        return waits

    def _commit(self, tok, reads, writes):
        s, c = tok
        for r in reads:
            if r.r.get(s, 0) < c:
                r.r[s] = c
        for w in writes:
            w.w = {s: c}
            w.r = {}

    def op(self, eng, fn, reads=(), writes=()):
        waits = self._collect(eng, reads, writes)
        if self.cur_cnt[eng] >= SEM_LIMIT:
            self.cur_sem[eng] = self._new_sem()
            self.cur_cnt[eng] = 0
        self.cur_cnt[eng] += 1
        tok = (self.cur_sem[eng], self.cur_cnt[eng])
        if eng not in self.same_sync:
            self.seen[eng][tok[0]] = tok[1]
        self.streams[eng].append((fn, waits, tok[0], 1))
        self._commit(tok, reads, writes)
        return tok

    def dma(self, eng, fns, reads=(), writes=(), key="dma"):
        if not isinstance(fns, (list, tuple)):
            fns = [fns]
        waits = self._collect(eng, reads, writes)
        ent = self.dma_sems.get(key)
        if ent is None or ent[1] + 16 * len(fns) >= SEM_LIMIT:
            ent = [self._new_sem(), 0]
            self.dma_sems[key] = ent
        for i, fn in enumerate(fns):
            ent[1] += 16
            self.streams[eng].append((fn, waits if i == 0 else [], ent[0], 16))
        tok = (ent[0], ent[1])
        self._commit(tok, reads, writes)
        self.all_dma_tokens[tok[0]] = tok[1]
        return tok

    def barrier(self):
        toks = []
        for s, c in self.all_dma_tokens.items():
            toks.append((s, c))
        for e in ENG_NAMES:
            if self.cur_cnt[e] > 0:
                toks.append((self.cur_sem[e], self.cur_cnt[e]))
        for e in ENG_NAMES:
            waits = []
            for s, c in toks:
                if self.seen[e].get(s, 0) < c:
                    waits.append((s, c))
                    self.seen[e][s] = c
            if waits:
                self.streams[e].append((None, waits, None, 0))

    def emit(self, nc):
        sems = [nc.alloc_semaphore(f"s{i}") for i in range(self.nsem)]
        streams = self.streams

        def run(eng, name):
            for fn, waits, incsem, incval in streams[name]:
                for s, c in waits:
                    eng.wait_ge(sems[s], c)
                if fn is None:
                    continue
                ins = fn(eng)
                if incsem is not None:
                    ins.then_inc(sems[incsem], incval)

        with nc.Block() as block:
            @block.tensor
            def _(e):
                run(e, "pe")

            @block.scalar
            def _(e):
                run(e, "act")

            @block.vector
            def _(e):
                run(e, "dve")

            @block.gpsimd
            def _(e):
                run(e, "pool")

            @block.sync
            def _(e):
                run(e, "sp")


def build_nc(debug_stop=None):
    nc = bass.Bass("TRN2", target_bir_lowering=False)
    P = Prog()

    def din(name, shape):
        return nc.dram_tensor(name, list(shape), F32, kind="ExternalInput").ap()

    def dout(name, shape):
        return nc.dram_tensor(name, list(shape), F32, kind="ExternalOutput").ap()

    x_d = din("xcat", [NT, D])
    xpre_d = din("xpre", [NTP, D])
    p_d = din("pcat", [NT, 256])
    sth_d = din("sth", [NSQ, 8, 128, 128])
    stg_d = din("stg", [NSQ, 4, 128, 256])
    w_in_d = din("w_in", [D, IN_COLS])
    w_br_d = din("w_branch", [2, 1024, D])
    w_out_d = din("w_out", [D, D])
    w_gu_d = din("w_gu", [D, 2 * DFF])
    w_dn_d = din("w_down", [DFF, D])
    w_ple_d = din("w_ple", [256, D])
    w_pg_d = din("w_pg", [D, D])
    ln_d = {k: din(k, [D]) for k in ("ln1", "ln2", "ln3", "ln_f")}
    hglb_d = din("hg_lbT", [128, 16])
    hgn_d = din("hg_norm", [128])
    glan_d = din("gla_norm", [256])
    wgk_d = din("w_gk", [16, 512])
    bgk_d = din("b_gkT", [128, 4])
    ident_d = din("ident", [128, 128])
    maskT_d = din("maskT", [128, 128])
    bmaskT_d = din("bmaskT", [64, 64])
    qmask_d = din("qmask", [128, 16 * 64])
    rowmask_d = din("rowmask", [64, 16])

    y_d = dout("ycat", [NT, D])
    hp_d = dout("hgrn_p", [8, 128, 128])
    gp_d = dout("gla_p", [4, 128, 256])
    hs_d = dout("hgrn_s", [NSQ, 8, 128, 128])
    gs_d = dout("gla_s", [NSQ, 4, 128, 256])

    base = 16512
    top = int(nc.sbuf_top)
    OFF_C = base
    SZ_C = 16384
    OFF_H = OFF_C + SZ_C
    SZ_H = KC * NT * 2
    OFF_R = OFF_H + SZ_H
    SZ_R = SZ_H
    OFF_X = OFF_R + SZ_R
    SZ_X = SZ_H + 5 * 8192
    OFF_W = OFF_X + SZ_X
    SZ_W = 3 * 16384
    assert OFF_W + SZ_W <= top, (OFF_W + SZ_W, top)

    cnt = [0]

    def at(off, shape, dt):
        cnt[0] += 1
        return nc.alloc_sbuf_tensor_at(f"t{cnt[0]}", list(shape), dt, offset=off)

    class Carver:
        def __init__(self, off, size):
            self.off = off
            self.end = off + size

        def take(self, shape, dt, nbytes=None):
            esz = 4 if dt == F32 else 2
            n = 1
            for s in shape[1:]:
                n *= s
            nb = n * esz if nbytes is None else nbytes
            nb = (nb + 31) // 32 * 32
            t = at(self.off, shape, dt)
            self.off += nb
            assert self.off <= self.end, "carver overflow"
            return t

    cc = Carver(OFF_C, SZ_C)
    ident = cc.take([128, 128], BF16)
    maskT = cc.take([128, 128], F32)
    bmaskT = cc.take([128, 64], F32)
    qmask = cc.take([128, 16, 64], BF16)
    rowmask = cc.take([128, 16], F32)
    hg_gain = cc.take([128, 128], F32)
    gla_gain = cc.take([128, 256], F32)
    hglb = cc.take([128, 16], F32)
    lb = cc.take([128, 8], F32)
    oml = cc.take([128, 8], F32)
    noml = cc.take([128, 8], F32)
    nbgk = cc.take([128, 4], F32)
    wgk = cc.take([128, 512], BF16)
    one1 = cc.take([128, 8], F32)
    sc_a = cc.take([128, 32], F32)
    sc_d = cc.take([128, 24], F32)
    sc_r = cc.take([128, 24], F32)
    nrm_s = cc.take([128, 8], F32)
    PSTATE_OFF = cc.off
    pst_h = at(PSTATE_OFF, [128, 8, 128], F32)
    pst_g = at(PSTATE_OFF + 4096, [128, 4, 256], F32)
    gamma_c = at(PSTATE_OFF, [128, D], F32)
    SB16B_OFF = PSTATE_OFF + 8192
    assert SB16B_OFF + 1024 <= OFF_C + SZ_C

    r_const = Res("const")
    r_small = Res("small")
    r_nrm = Res("nrm")
    r_nrmAB = [Res("nrmA"), Res("nrmB")]
    r_pst = [Res(f"pst{i}") for i in range(12)]
    r_gamma = Res("gamma")

    hT = at(OFF_H, [128, KC, NT], BF16)
    r_hT = Res("hT")
    brT = at(OFF_R, [128, KC, NT], BF16)
    r_brT = Res("brT")
    xs_ = Carver(OFF_X, SZ_X)
    T = [xs_.take([128, NT], F32) for _ in range(4)]
    r_Tg = [[Res(f"T{i}g{g}") for g in range(3)] for i in range(4)]
    QE = [xs_.take([128, NT], BF16) for _ in range(2)]
    r_QE = [Res("QE0"), Res("QE1")]
    KE = xs_.take([128, NT], BF16)
    r_KE = Res("KE")
    QEF = at(OFF_X + 4 * NT * 4, [128, NT], F32)
    r_QEFg = [Res(f"QEFg{g}") for g in range(3)]
    BSETS = [(T[1], r_Tg[1], T[2], r_Tg[2]), (T[0], r_Tg[0], QEF, r_QEFg)]
    KOUT = xs_.take([128, NT], BF16)
    r_KOUT = Res("KOUT")
    KOTOK = xs_.take([128, 9, 128], BF16)
    r_KOTOK = Res("KOTOK")
    ATOK_OFF = xs_.off
    ATOK = xs_.take([128, 9, 256], BF16)
    r_ATOK = Res("ATOK")
    ATOKF = at(ATOK_OFF, [128, 2, 512], F32)
    VTOK = xs_.take([128, 9, 512], BF16)
    r_VTOK = Res("VTOK")
    GSG = xs_.take([128, 9, 512], BF16)
    r_GSG = Res("GSG")
    NS0, NSB = 3, 2
    S0 = [xs_.take([128, 2, 256], F32) for _ in range(NS0)]
    r_S0 = [Res(f"S0_{i}") for i in range(NS0)]
    SB16 = [xs_.take([128, 2, 256], BF16), at(SB16B_OFF, [128, 2, 256], BF16)]
    r_SB16 = [Res(f"SB16_{i}") for i in range(NSB)]
    s0cnt = [0]
    QBLK = xs_.take([128, 16, 64], BF16)
    r_QBLK = Res("QBLK")
    KBLK = xs_.take([128, 16, 128], BF16)
    r_KBLK = Res("KBLK")
    QINS = [xs_.take([128, 64], BF16) for _ in range(2)]
    r_QINS = [Res("QINS0"), Res("QINS1")]
    ATTM8 = xs_.take([128, 8, 128], BF16)
    r_ATTM8 = Res("ATTM8")
    ATTMS = xs_.take([128, 64], BF16)
    r_ATTMS = Res("ATTMS")
    SBP8 = xs_.take([128, 8, 256], BF16)
    r_SBP8 = Res("SBP8")
    JUNKV = xs_.take([128, 256], BF16)
    r_JUNKV = Res("JUNKV")
    SM = [xs_.take([128, 64], F32) for _ in range(3)]
    r_SM = [Res(f"SM{i}") for i in range(3)]
    SCR = [xs_.take([128, 40], F32) for _ in range(2)]
    r_SCR = [Res("SCR0"), Res("SCR1")]
    nrm4 = xs_.take([128, 8], F32)
    r_nrm2 = Res("nrm2")
    GLR = xs_.take([128, NT], BF16)
    r_GLR = Res("GLR")
    xn = Carver(OFF_X, SZ_X)
    XT = [xn.take([128, D], F32) for _ in range(2)]
    r_XT = [Res("XT0"), Res("XT1")]
    HB = xn.take([128, D], BF16)
    r_HB = Res("HB")
    HB2 = xn.take([128, D], BF16)
    r_HB2 = Res("HB2")
    JUNK = xn.take([128, D], BF16)
    r_JUNK = Res("JUNK")
    gamma_x = xn.take([128, D], F32)
    mergedT = at(OFF_X, [128, KC, NT], BF16)
    r_mergedT = Res("mergedT")
    actT = at(OFF_X, [128, 12, NT], BF16)
    r_actT = Res("actT")
    XR = []
    for i in range(9):
        if i < 4:
            XR.append(at(OFF_R + i * 8192, [128, D], F32))
        else:
            XR.append(at(OFF_X + SZ_H + (i - 4) * 8192, [128, D], F32))
    r_XR = [Res(f"XR{i}") for i in range(9)]
    WS = [at(OFF_W + i * 16384, [128, 8192], BF16) for i in range(3)]
    r_WS = [Res(f"W{i}") for i in range(3)]
    wcur = [0]
    ring = {"slots": WS, "res": r_WS}

    def set_ring(offsets, nbytes):
        ring["slots"] = [at(off, [128, nbytes // 2], BF16) for off in offsets]
        ring["res"] = [Res(f"Wr{i}") for i in range(len(offsets))]
        wcur[0] = 0

    PSB = [nc.alloc_psum_tensor(f"ps{i}", [128, 512], F32) for i in range(8)]
    r_PS = [Res(f"ps{i}") for i in range(8)]
    r_SL = {b: [r_PS[b]] * 4 for b in range(3, 8)}
    ring_state = {"proj": 0}

    def next_bank(banks):
        i = banks[ring_state["proj"] % len(banks)]
        ring_state["proj"] += 1
        return PSB[i], r_PS[i]

    PROJ_BANKS = [0, 1, 2]
    ALL_BANKS = [0, 1, 2, 3, 4, 5, 6, 7]

    def MM(out, lhsT, rhs, start, stop, rd, wr):
        return P.op("pe", lambda e: e.matmul(out, lhsT=lhsT, rhs=rhs, start=start, stop=stop), rd, wr)

    def TR(out, in_, idn, rd, wr):
        return P.op("pe", lambda e: e.transpose(out=out, in_=in_, identity=idn), rd, wr)

    def ACT(out, in_, func, rd, wr, bias=None, scale=None, accum=None):
        kw = {}
        if bias is not None:
            kw["bias"] = bias
        if scale is not None:
            kw["scale"] = scale
        if accum is not None:
            kw["accum_out"] = accum
        return P.op("act", lambda e: e.activation(out=out, in_=in_, func=func, **kw), rd, wr)

    def TS(eng, out, in0, s1, s2, op0, op1, rd, wr):
        if op1 is None:
            return P.op(eng, lambda e: e.tensor_scalar(out=out, in0=in0, scalar1=s1, scalar2=None, op0=op0), rd, wr)
        return P.op(eng, lambda e: e.tensor_scalar(out=out, in0=in0, scalar1=s1, scalar2=s2, op0=op0, op1=op1), rd, wr)

    def TT(eng, out, in0, in1, op, rd, wr):
        return P.op(eng, lambda e: e.tensor_tensor(out=out, in0=in0, in1=in1, op=op), rd, wr)

    def STT(out, in0, scalar, in1, op0, op1, rd, wr):
        return P.op("dve", lambda e: e.scalar_tensor_tensor(out=out, in0=in0, scalar=scalar, in1=in1, op0=op0, op1=op1), rd, wr)

    def RECIP(out, in_, rd, wr):
        return P.op("dve", lambda e: e.reciprocal(out=out, in_=in_), rd, wr)

    def CP(eng, out, in_, rd, wr):
        if eng == "act":
            return P.op("act", lambda e: e.copy(out=out, in_=in_), rd, wr)
        return P.op(eng, lambda e: e.tensor_copy(out=out, in_=in_), rd, wr)

    def MEMSET(eng, out, val, rd, wr):
        return P.op(eng, lambda e: e.memset(out, val), rd, wr)

    def DMA(eng, out, in_, rd, wr, key):
        return P.dma(eng, lambda e: e.dma_start(out=out, in_=in_), rd, wr, key=key)

    def load_w(src, kcn, ncols, key="w"):
        i = wcur[0] % len(ring["slots"])
        wcur[0] += 1
        WS_, r_WS_ = ring["slots"], ring["res"]
        view = WS_[i][:, 0:kcn * ncols].rearrange("p (k n) -> p k n", n=ncols)
        srcv = src.rearrange("(kc p) n -> p kc n", p=128)
        fns = []
        step = 4
        for k0 in range(0, kcn, step):
            k1 = min(kcn, k0 + step)
            fns.append(lambda e, k0=k0, k1=k1: e.dma_start(out=view[:, k0:k1, :], in_=srcv[:, k0:k1, :]))
        P.dma("pool", fns, [], [r_WS_[i]], key=f"w{i}")
        return view, r_WS_[i]

    P.dma("pool", [lambda e: e.dma_start(out=ident[:], in_=ident_d),
                   lambda e: e.dma_start(out=qmask[:], in_=qmask_d.rearrange("p (j t) -> p j t", t=64)),
                   lambda e: e.dma_start(out=wgk[0:16, :], in_=wgk_d)],
          [], [r_const], key="const")
    P.dma("sp", [lambda e: e.dma_start(out=maskT[:], in_=maskT_d),
                 lambda e: e.dma_start(out=bmaskT[0:64, :], in_=bmaskT_d),
                 lambda e: e.dma_start(out=rowmask[0:64, :], in_=rowmask_d),
                 lambda e: e.dma_start(out=hg_gain[:], in_=hgn_d.partition_broadcast(128)),
                 lambda e: e.dma_start(out=gla_gain[:], in_=glan_d.partition_broadcast(128)),
                 lambda e: e.dma_start(out=hglb[:], in_=hglb_d),
                 lambda e: e.dma_start(out=nbgk[:], in_=bgk_d),
                 ],
          [], [r_const], key="const2")
    MEMSET("dve", one1[:], 1.0, [], [r_const])
    TT("dve", lb[:], hglb[:, 8:16], hglb[:, 0:8], ALU.subtract, [r_const], [r_const])
    ACT(lb[:], lb[:], AF.Exp, [r_const], [r_const])
    TS("dve", lb[:], lb[:], 1.0, None, ALU.add, None, [r_const], [r_const])
    RECIP(lb[:], lb[:], [r_const], [r_const])
    TS("dve", oml[:], lb[:], -1.0, 1.0, ALU.mult, ALU.add, [r_const], [r_const])
    TS("dve", noml[:], lb[:], -1.0, None, ALU.add, None, [r_const], [r_const])
    TS("dve", nbgk[:], nbgk[:], -1.0, None, ALU.mult, None, [r_const], [r_const])
    for i in range(8):
        MEMSET("dve", pst_h[:, i, :], 0.0, [], [r_pst[i]])
    for i in range(4):
        MEMSET("dve", pst_g[:, i, :], 0.0, [], [r_pst[8 + i]])


    ones_b = lambda n: one1[:, 0:1].broadcast_to([128, n])

    def rstd_from_ss(ss, rstd, n, dim, rd_extra):
        TS("dve", rstd, ss, 1.0 / dim, EPS, ALU.mult, ALU.add, [r_nrm] + rd_extra, [r_nrm])
        ACT(rstd, rstd, AF.Ln, [r_nrm], [r_nrm])
        ACT(rstd, rstd, AF.Exp, [r_nrm], [r_nrm], scale=-0.5)

    ncnt = [0]

    def norm_stage1(src_tile, r_src, ntok, gamma, r_gam):
        k_ = ncnt[0] % 2
        ncnt[0] += 1
        hb, r_hb = (HB, r_HB) if k_ == 0 else (HB2, r_HB2)
        r_n = r_nrmAB[k_]
        ss = nrm_s[0:ntok, 4 * k_:4 * k_ + 1]
        rstd = nrm_s[0:ntok, 4 * k_ + 1:4 * k_ + 2]
        ACT(JUNK[0:ntok, :], src_tile[0:ntok, :], AF.Square, [r_src], [r_JUNK, r_n], accum=ss)
        TS("dve", rstd, ss, 1.0 / D, EPS, ALU.mult, ALU.add, [r_n], [r_n])
        ACT(rstd, rstd, AF.Ln, [r_n], [r_n])
        ACT(rstd, rstd, AF.Exp, [r_n], [r_n], scale=-0.5)
        STT(hb[0:ntok, :], src_tile[0:ntok, :], rstd, gamma[0:ntok, :], ALU.mult, ALU.mult,
            [r_src, r_n, r_gam], [r_hb])
        return hb, r_hb

    def norm_stage2(hb, r_hb, ntok, dstT, r_dst, col0):
        for half in range(2):
            pb, r_pb = next_bank(ALL_BANKS)
            pbv = pb[:, :].bitcast(BF16).rearrange("p (k n) -> p k n", n=128)
            for k in range(8):
                kc = half * 8 + k
                TR(pbv[:, k, 0:ntok], hb[0:ntok, kc * 128:(kc + 1) * 128], ident[0:ntok, 0:ntok],
                   [r_hb, r_const], [r_pb])
            eng = "act" if half == 0 else "dve"
            CP(eng, dstT[:, half * 8:(half + 1) * 8, col0:col0 + ntok], pbv[:, 0:8, 0:ntok], [r_pb], [r_dst])

    def norm_pipeline(tile_srcs, tiles, gamma, r_gam, dstT, r_dst):
        prev = None
        for ti, (c0, nt) in enumerate(tiles):
            src, r_src = tile_srcs(ti, c0, nt)
            hb, r_hb = norm_stage1(src, r_src, nt, gamma, r_gam)
            if prev is not None:
                norm_stage2(*prev)
            prev = (hb, r_hb, nt, dstT, r_dst, c0)
        norm_stage2(*prev)

    def norm_phase_from_dram(src_d, tiles, gamma, r_gam, dstT, r_dst):
        def srcs(ti, c0, nt):
            xt, r_xt = XT[ti % 2], r_XT[ti % 2]
            DMA("sp", xt[0:nt, :], src_d[c0:c0 + nt, :], [], [r_xt], key=f"xt{ti % 2}")
            return xt, r_xt
        norm_pipeline(srcs, tiles, gamma, r_gam, dstT, r_dst)

    TILES_MAIN = [(i * 128, 128) for i in range(8)] + [(1024, 64)]
    TILES_PRE = [(i * 128, 128) for i in range(8)]
    TG_MAIN = [(0, 384), (384, 768), (768, 1088)]
    TG_PRE = [(0, 384), (384, 768), (768, 1024)]
    TGI = {0: 0, 384: 1, 768: 2}

    def fproj(wv, r_w, kcn, coff, srcT, r_src, kc_off, tgs, evac, ncol=128, banks=PROJ_BANKS):
        for (t0, t1) in tgs:
            pb, r_pb = next_bank(banks)
            for kc in range(kcn):
                MM(pb[0:ncol, 0:t1 - t0], wv[:, kc, coff:coff + ncol], srcT[:, kc_off + kc, t0:t1],
                   kc == 0, kc == kcn - 1, [r_w, r_src], [r_pb])
            evac(pb, r_pb, t0, t1)

    def tproj(wv, r_w, kcn, ncols, srcT, r_src, tiles, evac, banks=PROJ_BANKS):
        for ti, (c0, nt) in enumerate(tiles):
            pb, r_pb = next_bank(banks)
            for kc in range(kcn):
                MM(pb[0:nt, 0:ncols], srcT[:, kc, c0:c0 + nt], wv[:, kc, :], kc == 0, kc == kcn - 1,
                   [r_w, r_src], [r_pb])
            evac(pb, r_pb, ti, nt)

    def sig_inplace(buf_ap, rd, wr):
        TS("dve", buf_ap, buf_ap, 1.0, None, ALU.add, None, rd, wr)
        RECIP(buf_ap, buf_ap, wr, wr)

    def act_sigmoid(dst, src_psum, rd_src, r_dst):
        ACT(dst, src_psum, AF.Exp, rd_src, r_dst, scale=-1.0)
        ACT(dst, dst, AF.Ln, r_dst, r_dst, bias=1.0)
        ACT(dst, dst, AF.Exp, r_dst, r_dst, scale=-1.0)

    def prep_steps(main, par, bset=0):
        sq, tmp = T[0], T[3]
        kk, r_kk, lf, r_lf = BSETS[bset]
        scr, r_scr = SCR[par], r_SCR[par]
        cum3 = tmp[:, 0:NTP].rearrange("p (t c) -> p t c", c=128)
        lf3 = lf[:, 0:NTP].rearrange("p (t c) -> p t c", c=128)
        cm, cl, cb, = sc_a[:, 0:8], sc_a[:, 8:16], sc_a[:, 16:24]
        lfs = lf[:, NTP:NT].rearrange("p (j t) -> p j t", t=4)
        cs, dls, es = SM[0], SM[1], SM[2]
        cs3 = cs[:, :].rearrange("p (j t) -> p j t", t=4)
        dl3 = dls[:, :].rearrange("p (j t) -> p j t", t=4)
        refb = cs3[:, :, 2:3].broadcast_to([128, 16, 4])
        lastb = cs3[:, :, 3:4].broadcast_to([128, 16, 4])

        def p0():
            P.op("dve", lambda e: e.tensor_tensor_scan(out=tmp[:, 0:NTP], data0=ones_b(NTP), data1=lf[:, 0:NTP],
                                                      initial=0.0, op0=ALU.mult, op1=ALU.add),
                 [*r_lf, r_const], [*r_Tg[3]])

        def p1():
            CP("dve", cm, cum3[:, :, 64], [*r_Tg[3]], [r_small])
            CP("dve", cl, cum3[:, :, 127], [*r_Tg[3]], [r_small])
            MEMSET("dve", sc_a[:, 16:17], 0.0, [], [r_small])
            CP("dve", sc_a[:, 17:24], sc_a[:, 8:15], [r_small], [r_small])
            TT("dve", sc_d[:, 0:8], cm, cb, ALU.subtract, [r_small], [r_small])
            TT("dve", sc_d[:, 8:16], cl, cb, ALU.subtract, [r_small], [r_small])
            TT("dve", sc_d[:, 16:24], cl, cm, ALU.subtract, [r_small], [r_small])
            if main:
                TT("dve", lf3, cum3, cm.unsqueeze(2).broadcast_to([128, 8, 128]), ALU.subtract,
                   [*r_Tg[3], r_small], [*r_lf])
            else:
                TT("dve", lf3, cl.unsqueeze(2).broadcast_to([128, 8, 128]), cum3, ALU.subtract,
                   [*r_Tg[3], r_small], [*r_lf])

        def p2():
            ACT(scr[:, 0:24], sc_d[:, :], AF.Exp, [r_small], [r_scr])
            ACT(tmp[:, 0:NTP], lf[:, 0:NTP], AF.Exp, [*r_lf], [*r_Tg[3]])

        def p3():
            TT("pool", QE[par][:, 0:NTP], sq[:, 0:NTP], tmp[:, 0:NTP], ALU.mult, [*r_Tg[0], *r_Tg[3]], [r_QE[par]])

        def p4():
            ACT(tmp[:, 0:NTP], lf[:, 0:NTP], AF.Exp, [*r_lf], [*r_Tg[3]], scale=-1.0)

        def p5():
            TT("pool", KE[:, 0:NTP], kk[:, 0:NTP], tmp[:, 0:NTP], ALU.mult, [*r_kk, *r_Tg[3]], [r_KE])
            TT("pool", KOUT[:, 0:NTP].rearrange("p (t c) -> p t c", c=128),
               KE[:, 0:NTP].rearrange("p (t c) -> p t c", c=128),
               scr[:, 16:24].unsqueeze(2).broadcast_to([128, 8, 128]), ALU.mult, [r_KE, r_scr], [r_KOUT])

        def p5_prefix():
            TT("pool", KOUT[:, 0:NTP], kk[:, 0:NTP], tmp[:, 0:NTP], ALU.mult, [*r_kk, *r_Tg[3]], [r_KOUT])

        def q0():
            CP("pool", cs3[:, :, 0], lfs[:, :, 0], [*r_lf], [r_SM[0]])
            for t in range(1, 4):
                TT("pool", cs3[:, :, t], cs3[:, :, t - 1], lfs[:, :, t], ALU.add, [*r_lf, r_SM[0]], [r_SM[0]])
            TT("pool", dl3, cs3, refb, ALU.subtract, [r_SM[0]], [r_SM[1]])

        def q1():
            ACT(es[:, :], dls[:, :], AF.Exp, [r_SM[1]], [r_SM[2]])

        def q2():
            TT("pool", QE[par][:, NTP:NT], sq[:, NTP:NT], es[:, :], ALU.mult, [*r_Tg[0], r_SM[2]], [r_QE[par]])

        def q3():
            ACT(es[:, :], dls[:, :], AF.Exp, [r_SM[1]], [r_SM[2]], scale=-1.0)

        def q4():
            TT("pool", KE[:, NTP:NT], kk[:, NTP:NT], es[:, :], ALU.mult, [*r_kk, r_SM[2]], [r_KE])

        def q5():
            ACT(es[:, :], cs[:, :], AF.Exp, [r_SM[0]], [r_SM[2]])

        def q6():
            TT("pool", QINS[par][:, :], sq[:, NTP:NT], es[:, :], ALU.mult, [*r_Tg[0], r_SM[2]], [r_QINS[par]])
            TT("pool", dl3, lastb, cs3, ALU.subtract, [r_SM[0]], [r_SM[1]])

        def q7():
            ACT(es[:, :], dls[:, :], AF.Exp, [r_SM[1]], [r_SM[2]])
            ACT(scr[:, 24:40], cs3[:, :, 3], AF.Exp, [r_SM[0]], [r_scr])

        def q8():
            TT("pool", KOUT[:, NTP:NT], kk[:, NTP:NT], es[:, :], ALU.mult, [*r_kk, r_SM[2]], [r_KOUT])

        def tr():
            pb, r_pb = next_bank(PROJ_BANKS)
            pbv = pb[:, :].bitcast(BF16).rearrange("p (k n) -> p k n", n=128)
            for t in range(8):
                TR(pbv[:, t, :], KOUT[:, t * 128:(t + 1) * 128], ident[:, :], [r_KOUT, r_const], [r_pb])
            CP("act" if main else "dve", KOTOK[:, 0:8, :], pbv[:, 0:8, :], [r_pb], [r_KOTOK])
            if main:
                pb, r_pb = next_bank(PROJ_BANKS)
                pbv2 = pb[:, :].bitcast(BF16)
                TR(pbv2[0:64, 0:128], KOUT[:, NTP:NT], ident[:, :], [r_KOUT, r_const], [r_pb])
                CP("act", KOTOK[0:64, 8, :], pbv2[0:64, 0:128], [r_pb], [r_KOTOK])

        def pE_prefix():
            ACT(lf[:, 0:NTP], tmp[:, 0:NTP], AF.Exp, [*r_Tg[3]], [*r_lf], scale=-1.0, bias=tmp[:, NTP - 1:NTP])

        def pK_prefix():
            TT("pool", KOUT[:, 0:NTP], kk[:, 0:NTP], lf[:, 0:NTP], ALU.mult, [*r_kk, *r_lf], [r_KOUT])

        if main:
            return [p0, q0, p1, q1, p2, q2, p3, q3, p4, q4, p5, q5, q6, q7, q8, tr]
        return [p0, pE_prefix, pK_prefix, tr]

    def scan(main, par, V, vcol0, r_state, S_ap, br_chunk0, state_src, state_dst, mid=None, fillers=()):
        fillers = list(fillers)
        scr, r_scr = SCR[par], r_SCR[par]
        nch = V // 128

        def kslot(g, tt):
            if V == 128:
                b = 4 + g
                return PSB[b][:, tt * 128:(tt + 1) * 128], [r_PS[b]]
            b = 4 + 2 * g + tt // 2
            q0 = (tt % 2) * 256
            return PSB[b][:, q0:q0 + 256], [r_PS[b]]

        def oslot(g, tt):
            if V == 128:
                b = 6 + g
                return PSB[b][:, tt * 128:(tt + 1) * 128], [r_PS[b]]
            return kslot(g, tt)

        def slot(base_bank, g, tt):
            return oslot(g, tt)

        def epilogue(g):
            for tt in range(4):
                po, r_po = oslot(g, tt)
                ACT(JUNKV[:, 0:V], po, AF.Square, r_po, [r_JUNKV, r_nrm2], accum=nrm4[:, tt:tt + 1])
            rstd_batch(nrm4[:, 0:4], nrm4[:, 4:8], V)
            for tt in range(4):
                t = g * 4 + tt
                po, r_po = oslot(g, tt)
                STT(ATOK[:, t, 0:V], po, nrm4[:, 4 + tt:5 + tt], GSG[:, t, vcol0:vcol0 + V], ALU.mult, ALU.mult,
                    r_po + [r_nrm2, r_GSG], [r_ATOK])

        if not main:
            pk, r_pk = PSB[4][:, 0:V], [r_PS[4]]
            for t in range(8):
                MM(pk, KOTOK[:, t, :], VTOK[:, t, vcol0:vcol0 + V], t == 0, t == 7, [r_KOTOK, r_VTOK], r_pk)
            CP("dve", S_ap, pk, r_pk, [r_state])
            if mid is not None:
                mid(None)
            for f in fillers:
                f()
            return
        for g in range(2):
            pa, r_pa = PSB[3], r_PS[3]
            if main:
                for tt in range(4):
                    t = g * 4 + tt
                    cs_ = slice(t * 128, (t + 1) * 128)
                    MM(pa[:, tt * 128:(tt + 1) * 128], KE[:, cs_], QE[par][:, cs_], True, True, [r_KE, r_QE[par]], [r_pa])
            for tt in range(4):
                t = g * 4 + tt
                pk, r_pk = kslot(g, tt)
                MM(pk, KOTOK[:, t, :], VTOK[:, t, vcol0:vcol0 + V], True, True, [r_KOTOK, r_VTOK], r_pk)
            if main:
                TT("dve", ATTM8[:, g * 4:(g + 1) * 4, :], pa[:, :].rearrange("p (t n) -> p t n", n=128),
                   maskT[:, :].unsqueeze(1).broadcast_to([128, 4, 128]), ALU.mult, [r_pa, r_const], [r_ATTM8])
            for tt in range(4):
                t = g * 4 + tt
                pk, r_pk = kslot(g, tt)
                if main:
                    TS("dve", SBP8[:, t, 0:V], S_ap, scr[:, t:t + 1], None, ALU.mult, None, [r_state, r_scr], [r_SBP8])
                STT(S_ap, S_ap, scr[:, 8 + t:9 + t], pk, ALU.mult, ALU.add, [r_state, r_scr] + r_pk, [r_state])
        if mid is not None:
            mid(0)
        if main:
            for g in range(2):
                for tt in range(4):
                    t = g * 4 + tt
                    cs_ = slice(t * 128, (t + 1) * 128)
                    po, r_po = oslot(g, tt)
                    MM(po, ATTM8[:, t, :], VTOK[:, t, vcol0:vcol0 + V], True, False, [r_ATTM8, r_VTOK], r_po)
                    MM(po, QE[par][:, cs_], SBP8[:, t, 0:V], False, True, [r_QE[par], r_SBP8], r_po)
            epilogue(0)
            epilogue(1)
            if mid is not None:
                mid(1)
            for _ in range(4):
                if fillers:
                    fillers.pop(0)()
        if not main:
            for f in fillers:
                f()
            return
        DMA("sp", state_dst[0], S_ap, [r_state], [], key="pstout")
        vt = VTOK[0:64, 8, vcol0:vcol0 + V]
        pa, r_pa = PSB[3], r_SL[3][0]
        MM(pa[0:64, 0:64], KE[:, NTP:NT], QE[par][:, NTP:NT], True, True, [r_KE, r_QE[par]], [r_pa])
        TT("dve", ATTMS[0:64, :], pa[0:64, 0:64], bmaskT[0:64, :], ALU.mult, [r_pa, r_const], [r_ATTMS])
        TT("pool", KBLK[0:64, :, :], KOTOK[0:64, 8, :].unsqueeze(1).broadcast_to([64, 16, 128]),
           rowmask[0:64, :].unsqueeze(2).broadcast_to([64, 16, 128]), ALU.mult, [r_KOTOK, r_const], [r_KBLK])
        TT("pool", QBLK[:, :, :], QINS[par][:, :].unsqueeze(1).broadcast_to([128, 16, 64]), qmask[:, :, :], ALU.mult,
           [r_QINS[par], r_const], [r_QBLK])
        po = PSB[6][0:64, 0:V]
        r_po = [r_SL[6][0], r_SL[6][1]] if V == 256 else [r_SL[6][0]]
        MM(po, ATTMS[0:64, :], vt, True, False, [r_ATTMS, r_VTOK], r_po)
        base = s0cnt[0]
        s0cnt[0] += 8

        def s0_load(pi):
            bi = (base + pi) % NS0
            DMA("sp", S0[bi][:, :, 0:V], state_src[2 * pi:2 * pi + 2].rearrange("j k v -> k j v"), [], [r_S0[bi]],
                key=f"s0in{bi}")
        for pi in range(NS0 - 1):
            s0_load(pi)
        for pi in range(8):
            bi = (base + pi) % NS0
            r_s0 = r_S0[bi]
            ci = pi % NSB
            sbp, r_sb = SB16[ci], r_SB16[ci]
            CP("act", sbp[:, :, 0:V], S0[bi][:, :, 0:V], [r_s0], [r_sb])
            bk = 4 + pi % 2
            r_pk = [r_PS[bk]]
            for jj in range(2):
                j = 2 * pi + jj
                MM(po, QBLK[:, j, :], sbp[:, jj, 0:V], False, (j == 15), [r_QBLK, r_sb], r_po)
            for jj in range(2):
                j = 2 * pi + jj
                MM(PSB[bk][:, jj * V:(jj + 1) * V], KBLK[0:64, j, :], vt, True, True, [r_KBLK, r_VTOK], r_pk)
            for jj in range(2):
                j = 2 * pi + jj
                s0 = S0[bi][:, jj, 0:V]
                STT(s0, s0, scr[:, 24 + j:25 + j], PSB[bk][:, jj * V:(jj + 1) * V], ALU.mult, ALU.add,
                    [r_s0, r_scr] + r_pk, [r_s0])
            if pi + NS0 - 1 < 8:
                s0_load(pi + NS0 - 1)
            DMA("sp", state_dst[1][2 * pi:2 * pi + 2].rearrange("j k v -> k j v"), S0[bi][:, :, 0:V], [r_s0], [],
                key=f"s0out{bi}")
            for _ in range(2):
                if fillers:
                    fillers.pop(0)()
        while fillers:
            fillers.pop(0)()
        ACT(JUNKV[0:64, 0:V], po, AF.Square, r_po, [r_JUNKV, r_nrm2], accum=nrm4[0:64, 0:1])
        rstd_batch(nrm4[0:64, 0:1], nrm4[0:64, 4:5], V)
        STT(ATOK[0:64, 8, 0:V], po, nrm4[0:64, 4:5], GSG[0:64, 8, vcol0:vcol0 + V], ALU.mult, ALU.mult,
            r_po + [r_nrm2, r_GSG], [r_ATOK])
        for c in range(nch):
            pb, r_pb = next_bank(PROJ_BANKS)
            pbv = pb[:, :].bitcast(BF16).rearrange("p (k n) -> p k n", n=128)
            for t in range(8):
                TR(pbv[:, t, :], ATOK[:, t, c * 128:(c + 1) * 128], ident[:, :], [r_ATOK, r_const], [r_pb])
            CP("act", brT[:, br_chunk0 + c, 0:NTP].rearrange("p (t n) -> p t n", n=128), pbv[:, 0:8, :], [r_pb], [r_brT])
            pb2, r_pb2 = next_bank(PROJ_BANKS)
            pbv2 = pb2[:, :].bitcast(BF16)
            TR(pbv2[:, 0:64], ATOK[0:64, 8, c * 128:(c + 1) * 128], ident[0:64, 0:64], [r_ATOK, r_const], [r_pb2])
            CP("act", brT[:, br_chunk0 + c, NTP:NT], pbv2[:, 0:64], [r_pb2], [r_brT])

    def rstd_batch(ss, rstd, dim):
        TS("dve", rstd, ss, 1.0 / dim, EPS, ALU.mult, ALU.add, [r_nrm2], [r_nrm2])
        ACT(rstd, rstd, AF.Ln, [r_nrm2], [r_nrm2])
        ACT(rstd, rstd, AF.Exp, [r_nrm2], [r_nrm2], scale=-0.5)

    def pipeline_prefix(jobs):
        n = len(jobs)
        jobs[0][0]()
        if n > 1:
            jobs[1][0]()
        for st in jobs[0][1]():
            st()
        for i in range(n):
            nx = jobs[i + 1][1]() if i + 1 < n else None
            if nx:
                nx[0]()
            jobs[i][2](None, [])
            if nx:
                nx[1]()
            if i + 2 < n:
                jobs[i + 2][0]()
            if nx:
                nx[2]()
                nx[3]()

    def pipeline(jobs, pre_done=False):
        n = len(jobs)
        if not pre_done:
            jobs[0][0]()
            for st in jobs[0][1]():
                st()
        for i in range(n):
            if i + 1 < n:
                jobs[i][2](jobs[i + 1][0], jobs[i + 1][1]())
            else:
                jobs[i][2](None, [])

    def gate_evac(gain_ap, nh, vh, fillers=None):
        def ev_g(pb, r_pb, ti, nt):
            e = ATOKF[0:nt, ti % 2, :]
            r_e = [r_ATOK]
            act_sigmoid(e, pb[0:nt, 0:512], [r_pb], r_e)
            TT("dve", e, e, pb[0:nt, 0:512], ALU.mult, [*r_e, r_pb], [*r_e])
            TT("pool", GSG[0:nt, ti, :].rearrange("p (h v) -> p h v", v=vh),
               e.rearrange("p (h v) -> p h v", v=vh),
               gain_ap[0:nt, :].unsqueeze(1).broadcast_to([nt, nh, vh]), ALU.mult,
               [*r_e, r_const], [r_GSG])
            if fillers:
                for _ in range(2):
                    if fillers:
                        fillers.pop(0)()
        return ev_g

    def ev_v(pb, r_pb, ti, nt):
        CP("act", VTOK[0:nt, ti, :], pb[0:nt, 0:512], [r_pb], [r_VTOK])

    hcount = [0]

    nxt_blk = {}

    def first_block(col0):
        if "blk" in nxt_blk:
            return nxt_blk.pop("blk")
        return load_w(w_in_d[:, col0:col0 + 512], KC, 512)

    def hgrn_half(main, hh, srcT, r_src, tiles, tgs, next_col=None):
        wv, r_w = first_block(C_HI + hh * 512)
        if main:
            wgt, r_wgt = load_w(w_in_d[:, C_HG + hh * 512:C_HG + (hh + 1) * 512], KC, 512)
        tproj(wv, r_w, KC, 512, srcT, r_src, tiles, ev_v)
        if main:
            wq, r_wq = load_w(w_in_d[:, C_HQ + hh * 512:C_HQ + (hh + 1) * 512], KC, 512)
        wf, r_wf = load_w(w_in_d[:, C_HF + hh * 512:C_HF + (hh + 1) * 512], KC, 512)

        def half_start(jobs):
            jobs[0][0]()
            fl = list(jobs[0][1]())
            tproj(wgt, r_wgt, KC, 512, srcT, r_src, tiles, gate_evac(hg_gain, 4, 128, fl))
            while fl:
                fl.pop(0)()
            if next_col is not None:
                nxt_blk["blk"] = load_w(w_in_d[:, next_col:next_col + 512], KC, 512)
        if not main and next_col is not None:
            nxt_blk["blk"] = load_w(w_in_d[:, next_col:next_col + 512], KC, 512)
        jobs = []
        for hl in range(4):
            h = hh * 4 + hl
            par = hcount[0] % 2
            hcount[0] += 1

            bset = 0 if main else hl % 2

            def A(part=None, hl=hl, h=h, bset=bset):
                if main and part in (None, 0):
                    def ev_q(pb, r_pb, t0, t1):
                        n = t1 - t0
                        g = TGI[t0]
                        e = T[0][:, t0:t1]
                        act_sigmoid(e, pb[:, 0:n], [r_pb], [r_Tg[0][g]])
                        STT(e, pb[:, 0:n], 128.0 ** -0.5, e, ALU.mult, ALU.mult, [r_pb, r_Tg[0][g]], [r_Tg[0][g]])
                    fproj(wq, r_wq, KC, hl * 128, srcT, r_src, 0, tgs, ev_q)
                if part == 0:
                    return

                kkb, r_kkb, lfb, r_lfb = BSETS[bset]

                def ev_f(pb, r_pb, t0, t1):
                    n = t1 - t0
                    g = TGI[t0]
                    e = kkb[:, t0:t1]
                    act_sigmoid(e, pb[:, 0:n], [r_pb], [r_kkb[g]])
                    ACT(lfb[:, t0:t1], e, AF.Ln, [r_kkb[g], r_const], [r_lfb[g]], scale=oml[:, h:h + 1], bias=lb[:, h:h + 1])
                    TS("dve", e, e, noml[:, h:h + 1], oml[:, h:h + 1], ALU.mult, ALU.add,
                       [r_kkb[g], r_const], [r_kkb[g]])
                fproj(wf, r_wf, KC, hl * 128, srcT, r_src, 0, tgs, ev_f)

            def B(par=par, bset=bset):
                return prep_steps(main, par, bset)

            def C(mid, fillers, hl=hl, h=h, par=par):
                scan(main, par, 128, hl * 128, r_pst[h], pst_h[:, h, :], h,
                     sth_d[:, h] if main else None, (hp_d[h], hs_d[:, h]) if main else None, mid, fillers)
            jobs.append((A, B, C))
        if main:
            half_start(jobs)
            pipeline(jobs, pre_done=True)
        else:
            pipeline_prefix(jobs)

    def gla_prep_glr(srcT, r_src, tgs):
        wv, r_w = load_w(w_in_d[:, C_GLR:C_GLR + 16], KC, 16)

        def ev(pb, r_pb, t0, t1):
            CP("act", GLR[0:16, t0:t1], pb[0:16, 0:t1 - t0], [r_pb], [r_GLR])
        fproj(wv, r_w, KC, 0, srcT, r_src, 0, tgs, ev, ncol=16)

    def gla_half(main, hh, srcT, r_src, tiles, tgs, next_col=None):
        wv, r_w = first_block(C_GV + hh * 512)
        if main:
            wgt, r_wgt = load_w(w_in_d[:, C_GG + hh * 512:C_GG + (hh + 1) * 512], KC, 512)
        tproj(wv, r_w, KC, 512, srcT, r_src, tiles, ev_v)
        if main:
            wq, r_wq = load_w(w_in_d[:, C_GQ + hh * 256:C_GQ + (hh + 1) * 256], KC, 256)
        wk, r_wk = load_w(w_in_d[:, C_GK + hh * 256:C_GK + (hh + 1) * 256], KC, 256)

        def half_start(jobs):
            jobs[0][0]()
            fl = list(jobs[0][1]())
            tproj(wgt, r_wgt, KC, 512, srcT, r_src, tiles, gate_evac(gla_gain, 2, 256, fl))
            while fl:
                fl.pop(0)()
            if next_col is not None:
                nxt_blk["blk"] = load_w(w_in_d[:, next_col:next_col + 512], KC, 512)
        if not main and next_col is not None:
            nxt_blk["blk"] = load_w(w_in_d[:, next_col:next_col + 512], KC, 512)
        jobs = []
        for hl in range(2):
            h = hh * 2 + hl
            par = hcount[0] % 2
            hcount[0] += 1

            bset = 0 if main else hl % 2

            def A(part=None, hl=hl, h=h, bset=bset):
                kkb, r_kkb, lfb, r_lfb = BSETS[bset]
                if main and part in (None, 0):
                    def ev_q(pb, r_pb, t0, t1):
                        P.op("act", lambda e, o=T[0][:, t0:t1], i=pb[:, 0:t1 - t0]: e.mul(out=o, in_=i, mul=128.0 ** -0.5),
                             [r_pb], [r_Tg[0][TGI[t0]]])
                    fproj(wq, r_wq, KC, hl * 128, srcT, r_src, 0, tgs, ev_q)
                if part == 0:
                    return

                def ev_k(pb, r_pb, t0, t1):
                    CP("act", kkb[:, t0:t1], pb[:, 0:t1 - t0], [r_pb], [r_kkb[TGI[t0]]])
                fproj(wk, r_wk, KC, hl * 128, srcT, r_src, 0, tgs, ev_k)
                for (t0, t1) in tgs:
                    pb, r_pb = next_bank(PROJ_BANKS)
                    n = t1 - t0
                    MM(pb[:, 0:n], wgk[0:16, h * 128:(h + 1) * 128], GLR[0:16, t0:t1], True, True, [r_const, r_GLR], [r_pb])
                    g = TGI[t0]
                    e = lfb[:, t0:t1]
                    ACT(e, pb[:, 0:n], AF.Exp, [r_pb, r_const], [r_lfb[g]], scale=-1.0, bias=nbgk[:, h:h + 1])
                    ACT(e, e, AF.Ln, [r_lfb[g]], [r_lfb[g]], bias=1.0)
                    TS("dve", e, e, -1.0 / 16.0, None, ALU.mult, None, [r_lfb[g]], [r_lfb[g]])

            def B(par=par, bset=bset):
                return prep_steps(main, par, bset)

            def C(mid, fillers, hl=hl, h=h, par=par):
                scan(main, par, 256, hl * 256, r_pst[8 + h], pst_g[:, h, :], 8 + 2 * h,
                     stg_d[:, h] if main else None, (gp_d[h], gs_d[:, h]) if main else None, mid, fillers)
            jobs.append((A, B, C))
        if main:
            half_start(jobs)
            pipeline(jobs, pre_done=True)
        else:
            pipeline_prefix(jobs)

    def dbg_finish(kind):
        if kind == "xr":
            for ti, (c0, nt) in enumerate(TILES_MAIN):
                DMA("sp", y_d[c0:c0 + nt, :], XR[ti][0:nt, :], [r_XR[ti]], [], key="yout")
        else:
            dbg_d = nc.dram_tensor("dbg", [128, KC, NT], F32, kind="ExternalOutput").ap()
            src = {"brT": brT, "mergedT": mergedT, "hT": hT}[kind]
            P.dma("pool", [lambda e, k=k: e.dma_start(out=dbg_d[:, k * 4:(k + 1) * 4, :], in_=src[:, k * 4:(k + 1) * 4, :]) for k in range(4)],
                  [], [], key="dbgout")
        P.barrier()
        P.emit(nc)
        return nc

    DMA("sp", gamma_x[:], ln_d["ln1"].partition_broadcast(128), [], [r_gamma], key="gamma")
    nxt_blk["blk"] = load_w(w_in_d[:, C_HI:C_HI + 512], KC, 512)
    norm_phase_from_dram(xpre_d, TILES_PRE, gamma_x, r_gamma, hT, r_hT)
    P.barrier()
    hgrn_half(False, 0, hT, r_hT, TILES_PRE, TG_PRE, next_col=C_HI + 512)
    hgrn_half(False, 1, hT, r_hT, TILES_PRE, TG_PRE, next_col=C_GV)
    gla_prep_glr(hT, r_hT, TG_PRE)
    gla_half(False, 0, hT, r_hT, TILES_PRE, TG_PRE, next_col=C_GV + 512)
    gla_half(False, 1, hT, r_hT, TILES_PRE, TG_PRE)
    P.barrier()
    DMA("sp", gamma_x[:], ln_d["ln1"].partition_broadcast(128), [], [r_gamma], key="gamma")
    nxt_blk["blk"] = load_w(w_in_d[:, C_HI:C_HI + 512], KC, 512)
    norm_phase_from_dram(x_d, TILES_MAIN, gamma_x, r_gamma, hT, r_hT)
    P.barrier()
    hgrn_half(True, 0, hT, r_hT, TILES_MAIN, TG_MAIN, next_col=C_HI + 512)
    hgrn_half(True, 1, hT, r_hT, TILES_MAIN, TG_MAIN, next_col=C_GV)
    gla_prep_glr(hT, r_hT, TG_MAIN)
    gla_half(True, 0, hT, r_hT, TILES_MAIN, TG_MAIN, next_col=C_GV + 512)
    gla_half(True, 1, hT, r_hT, TILES_MAIN, TG_MAIN)
    P.barrier()
    if debug_stop == "B":
        return dbg_finish("brT")
    set_ring([OFF_W, OFF_W + 16384, OFF_W + 32768, OFF_X + SZ_H + 8192, OFF_X + SZ_H + 24576], 16384)
    G0 = at(OFF_X + SZ_H, [128, 384], F32)
    G1 = at(OFF_X + SZ_H + 1536, [128, 384], F32)
    M0 = at(OFF_X + SZ_H + 3072, [128, 384], F32)
    r_G0, r_G1, r_M0 = Res("G0"), Res("G1"), Res("M0")
    for q in range(4):
        wg0, r_wg0 = load_w(w_in_d[:, C_MG + q * 512:C_MG + (q + 1) * 512], KC, 512)
        wg1, r_wg1 = load_w(w_in_d[:, C_MG + 2048 + q * 512:C_MG + 2048 + (q + 1) * 512], KC, 512)
        i = wcur[0] % len(ring["slots"])
        wcur[0] += 1
        wbv = ring["slots"][i][:, 0:16 * 512].rearrange("p (k n) -> p k n", n=512)
        r_wb = ring["res"][i]
        fns = []
        for n in range(2):
            for k0 in (0, 4):
                fns.append(lambda e, n=n, k0=k0, q=q, wbv=wbv: e.dma_start(
                    out=wbv[:, n * 8 + k0:n * 8 + k0 + 4, :],
                    in_=w_br_d[n, :, q * 512:(q + 1) * 512].rearrange("(kc p) n -> p kc n", p=128)[:, k0:k0 + 4, :]))
        P.dma("pool", fns, [], [r_wb], key=f"w{i}")
        for j in range(4):
            dc = q * 4 + j
            for (t0, t1) in TG_MAIN:
                n = t1 - t0
                pg0, r_pg0 = next_bank(ALL_BANKS)
                for kc in range(KC):
                    MM(pg0[:, 0:n], wg0[:, kc, j * 128:(j + 1) * 128], hT[:, kc, t0:t1], kc == 0, kc == KC - 1, [r_wg0, r_hT], [r_pg0])
                pg1, r_pg1 = next_bank(ALL_BANKS)
                for kc in range(KC):
                    MM(pg1[:, 0:n], wg1[:, kc, j * 128:(j + 1) * 128], hT[:, kc, t0:t1], kc == 0, kc == KC - 1, [r_wg1, r_hT], [r_pg1])
                pu0, r_pu0 = next_bank(ALL_BANKS)
                for kc in range(8):
                    MM(pu0[:, 0:n], wbv[:, kc, j * 128:(j + 1) * 128], brT[:, kc, t0:t1], kc == 0, kc == 7, [r_wb, r_brT], [r_pu0])
                pu1, r_pu1 = next_bank(ALL_BANKS)
                for kc in range(8):
                    MM(pu1[:, 0:n], wbv[:, 8 + kc, j * 128:(j + 1) * 128], brT[:, 8 + kc, t0:t1], kc == 0, kc == 7, [r_wb, r_brT], [r_pu1])
                ACT(G0[:, 0:n], pg0[:, 0:n], AF.Exp, [r_pg0], [r_G0], scale=-1.0)
                sig_inplace(G0[:, 0:n], [r_G0], [r_G0])
                ACT(G1[:, 0:n], pg1[:, 0:n], AF.Exp, [r_pg1], [r_G1], scale=-1.0)
                sig_inplace(G1[:, 0:n], [r_G1], [r_G1])
                TT("dve", M0[:, 0:n], G0[:, 0:n], pu0[:, 0:n], ALU.mult, [r_G0, r_pu0], [r_M0])
                TT("dve", G1[:, 0:n], G1[:, 0:n], pu1[:, 0:n], ALU.mult, [r_G1, r_pu1], [r_G1])
                TT("dve", mergedT[:, dc, t0:t1], M0[:, 0:n], G1[:, 0:n], ALU.add, [r_M0, r_G1], [r_mergedT])
    P.barrier()
    if debug_stop == "B6":
        return dbg_finish("mergedT")
    set_ring([OFF_W, OFF_W + 16384, OFF_W + 32768], 16384)
    for ti, (c0, nt) in enumerate(TILES_MAIN):
        DMA("sp", XR[ti][0:nt, :], x_d[c0:c0 + nt, :], [], [r_XR[ti]], key="xr")
    for cb in range(4):
        wv, r_w = load_w(w_out_d[:, cb * 512:(cb + 1) * 512], KC, 512)

        def ev_o(pb, r_pb, ti, nt, cb=cb):
            xv = XR[ti][0:nt, cb * 512:(cb + 1) * 512]
            TT("dve", xv, xv, pb[0:nt, 0:512], ALU.add, [r_XR[ti], r_pb], [r_XR[ti]])
        tproj(wv, r_w, KC, 512, mergedT, r_mergedT, TILES_MAIN, ev_o, banks=ALL_BANKS)
    P.barrier()
    if debug_stop == "C":
        return dbg_finish("xr")
    HB = at(OFF_X, [128, D], BF16)
    JUNK = at(OFF_X + 4096, [128, D], BF16)
    HB2 = at(OFF_X + 8192, [128, D], BF16)
    r_HB, r_JUNK, r_HB2 = Res("HBb"), Res("JUNKb"), Res("HB2b")

    def norm_resident(lnname, dstT, r_dst):
        DMA("sp", gamma_c[:], ln_d[lnname].partition_broadcast(128), [], [r_gamma], key="gamma")
        norm_pipeline(lambda ti, c0, nt: (XR[ti], r_XR[ti]), TILES_MAIN, gamma_c, r_gamma, dstT, r_dst)

    set_ring([OFF_W, OFF_W + 12288, OFF_W + 24576, OFF_W + 36864], 12288)
    ffn_pre = [load_w(w_gu_d[:, 0:384], KC, 384), load_w(w_gu_d[:, DFF:DFF + 384], KC, 384)]
    norm_resident("ln2", hT, r_hT)
    P.barrier()
    if debug_stop == "N2":
        return dbg_finish("hT")
    E0 = at(OFF_X + 26112, [128, 384], F32)
    E1 = at(OFF_X + 26112 + 1536, [128, 384], F32)
    r_E = [Res("E0"), Res("E1")]
    Eb = [E0, E1]
    ecnt = [0]
    fc0 = 0
    for gs in (12, 12, 12, 8):
        blocks = [3, 3, 3, 3] if gs == 12 else [3, 3, 2]
        c0_ = 0
        for bs in blocks:
            cg = (fc0 + c0_) * 128
            if ffn_pre:
                (wg, r_wg), (wu, r_wu) = ffn_pre
                ffn_pre = None
            else:
                wg, r_wg = load_w(w_gu_d[:, cg:cg + bs * 128], KC, bs * 128)
                wu, r_wu = load_w(w_gu_d[:, DFF + cg:DFF + cg + bs * 128], KC, bs * 128)
            for j in range(bs):
                c = c0_ + j
                for (t0, t1) in TG_MAIN:
                    n = t1 - t0
                    pg, r_pg = next_bank(ALL_BANKS)
                    for kc in range(KC):
                        MM(pg[:, 0:n], wg[:, kc, j * 128:(j + 1) * 128], hT[:, kc, t0:t1], kc == 0, kc == KC - 1, [r_wg, r_hT], [r_pg])
                    pu, r_pu = next_bank(ALL_BANKS)
                    for kc in range(KC):
                        MM(pu[:, 0:n], wu[:, kc, j * 128:(j + 1) * 128], hT[:, kc, t0:t1], kc == 0, kc == KC - 1, [r_wu, r_hT], [r_pu])
                    e, r_e = Eb[ecnt[0] % 2], r_E[ecnt[0] % 2]
                    ecnt[0] += 1
                    ACT(e[:, 0:n], pg[:, 0:n], AF.Exp, [r_pg], [r_e], scale=-1.0)
                    sig_inplace(e[:, 0:n], [r_e], [r_e])
                    TT("dve", e[:, 0:n], e[:, 0:n], pg[:, 0:n], ALU.mult, [r_e, r_pg], [r_e])
                    TT("dve", actT[:, c, t0:t1], e[:, 0:n], pu[:, 0:n], ALU.mult, [r_e, r_pu], [r_actT])
            c0_ += bs
        for cb in range(4):
            wd, r_wd = load_w(w_dn_d[fc0 * 128:(fc0 + gs) * 128, cb * 512:(cb + 1) * 512], gs, 512)

            def ev_d(pb, r_pb, ti, nt, cb=cb):
                xv = XR[ti][0:nt, cb * 512:(cb + 1) * 512]
                TT("dve", xv, xv, pb[0:nt, 0:512], ALU.add, [r_XR[ti], r_pb], [r_XR[ti]])
            tproj(wd, r_wd, gs, 512, actT, r_actT, TILES_MAIN, ev_d, banks=ALL_BANKS)
        fc0 += gs
    P.barrier()
    if debug_stop == "D":
        return dbg_finish("xr")
    set_ring([OFF_W, OFF_W + 16384, OFF_W + 32768], 16384)
    norm_resident("ln3", hT, r_hT)
    PTOK = at(OFF_X + 12288, [128, 9, 256], BF16)
    r_PTOK = Res("PTOK")
    pT = at(OFF_X + 12288 + 4608, [128, 2, NT], BF16)
    r_pT = Res("pT")
    WPLE = at(OFF_X + 12288 + 4608 + 4352, [128, 2, D], BF16)
    r_WPLE = Res("WPLE")
    SGs = [at(OFF_X + 12288 + 4608 + 4352 + 8192 + i * 2048, [128, 512], F32) for i in range(2)]
    r_SGs = [Res(f"SG{i}") for i in range(2)]
    sgc = [0]
    YB = [at(OFF_X + 8192 + i * 8192, [128, D], F32) for i in range(3)]
    r_YB = [Res(f"YB{i}") for i in range(3)]
    assert 12288 + 4608 + 4352 + 8192 + 2 * 2048 <= SZ_H
    P.dma("pool", [lambda e: e.dma_start(out=PTOK[:, 0:8, :], in_=p_d[0:NTP, :].rearrange("(t p) c -> p t c", p=128)),
                   lambda e: e.dma_start(out=PTOK[0:64, 8, :], in_=p_d[NTP:NT, :]),
                   lambda e: e.dma_start(out=WPLE[:], in_=w_ple_d.rearrange("(kc p) n -> p kc n", p=128))],
          [], [r_PTOK, r_WPLE], key="ptok")
    for kc in range(2):
        pb, r_pb = next_bank(ALL_BANKS)
        pbv = pb[:, :].bitcast(BF16).rearrange("p (k n) -> p k n", n=128)
        for t in range(8):
            TR(pbv[:, t, :], PTOK[:, t, kc * 128:(kc + 1) * 128], ident[:, :], [r_PTOK, r_const], [r_pb])
        CP("act", pT[:, kc, 0:NTP].rearrange("p (t n) -> p t n", n=128), pbv[:, 0:8, :], [r_pb], [r_pT])
        pb, r_pb = next_bank(ALL_BANKS)
        pbv2 = pb[:, :].bitcast(BF16)
        TR(pbv2[:, 0:64], PTOK[0:64, 8, kc * 128:(kc + 1) * 128], ident[0:64, 0:64], [r_PTOK, r_const], [r_pb])
        CP("act", pT[:, kc, NTP:NT], pbv2[:, 0:64], [r_pb], [r_pT])
    wpg = [load_w(w_pg_d[:, cb * 512:(cb + 1) * 512], KC, 512) for cb in range(3)]
    for cb in range(4):
        if cb == 1:
            wpg.append(load_w(w_pg_d[:, 3 * 512:4 * 512], KC, 512))
        wv, r_w = wpg[cb]
        for ti, (c0, nt) in enumerate(TILES_MAIN):
            pg, r_pg = next_bank(ALL_BANKS)
            for kc in range(KC):
                MM(pg[0:nt, 0:512], hT[:, kc, c0:c0 + nt], wv[:, kc, :], kc == 0, kc == KC - 1, [r_w, r_hT], [r_pg])
            pp, r_pp = next_bank(ALL_BANKS)
            for kc in range(2):
                MM(pp[0:nt, 0:512], pT[:, kc, c0:c0 + nt], WPLE[:, kc, cb * 512:(cb + 1) * 512], kc == 0, kc == 1, [r_WPLE, r_pT], [r_pp])
            SG, r_SG = SGs[sgc[0] % 2], r_SGs[sgc[0] % 2]
            sgc[0] += 1
            ACT(SG[0:nt, :], pg[0:nt, 0:512], AF.Exp, [r_pg], [r_SG], scale=-1.0)
            ACT(SG[0:nt, :], SG[0:nt, :], AF.Ln, [r_SG], [r_SG], bias=1.0)
            ACT(SG[0:nt, :], SG[0:nt, :], AF.Exp, [r_SG], [r_SG], scale=-1.0)
            TT("dve", SG[0:nt, :], SG[0:nt, :], pp[0:nt, 0:512], ALU.mult, [r_SG, r_pp], [r_SG])
            xv = XR[ti][0:nt, cb * 512:(cb + 1) * 512]
            TT("pool", xv, xv, SG[0:nt, :], ALU.add, [r_XR[ti], r_SG], [r_XR[ti]])
    P.barrier()
    if debug_stop == "E":
        return dbg_finish("xr")
    DMA("sp", gamma_c[:], ln_d["ln_f"].partition_broadcast(128), [r_gamma], [r_gamma], key="gamma")
    for ti, (c0, nt) in enumerate(TILES_MAIN):
        k_ = ti % 2
        r_n = r_nrmAB[k_]
        ss = nrm_s[0:nt, 4 * k_:4 * k_ + 1]
        rstd = nrm_s[0:nt, 4 * k_ + 1:4 * k_ + 2]
        ACT(JUNK[0:nt, :], XR[ti][0:nt, :], AF.Square, [r_XR[ti]], [r_JUNK, r_n], accum=ss)
        TS("dve", rstd, ss, 1.0 / D, EPS, ALU.mult, ALU.add, [r_n], [r_n])
        ACT(rstd, rstd, AF.Ln, [r_n], [r_n])
        ACT(rstd, rstd, AF.Exp, [r_n], [r_n], scale=-0.5)
        yb, r_yb = YB[ti % 3], r_YB[ti % 3]
        STT(yb[0:nt, :], XR[ti][0:nt, :], rstd, gamma_c[0:nt, :], ALU.mult, ALU.mult, [r_XR[ti], r_n, r_gamma], [r_yb])
        DMA("sp", y_d[c0:c0 + nt, :], yb[0:nt, :], [r_yb], [], key="yout")
    P.barrier()
    P.emit(nc)
    return nc


_NC_CACHE = {}


def _consts():
    ident = np.eye(128, dtype=np.float32)
    s = np.arange(128)
    maskT = (s[:, None] <= s[None, :]).astype(np.float32)
    s4 = np.arange(64)
    same = (s4[:, None] // 4) == (s4[None, :] // 4)
    bmaskT = (same & (s4[:, None] <= s4[None, :])).astype(np.float32)
    qm = ((np.arange(64)[None, :] // 4) == np.arange(16)[:, None]).astype(np.float32)
    qmask = np.broadcast_to(qm.reshape(1, 16 * 64), (128, 16 * 64)).copy()
    rowmask = ((np.arange(64)[:, None] // 4) == np.arange(16)[None, :]).astype(np.float32)
    return dict(ident=ident, maskT=maskT, bmaskT=bmaskT, qmask=qmask, rowmask=rowmask)


def make_in_maps(x_prompt, x_sample, state_hgrn, state_gla, p_prompt, p_sample, hg_lb, ln1, w_in,
                 hg_norm, gla_w_gk, gla_b_gk, gla_norm, w_branch, w_out, ln2, w_gu, w_down, ln3,
                 w_ple, w_pg, ln_f):
    f = lambda a: np.ascontiguousarray(np.asarray(a, dtype=np.float32))
    x_prompt, x_sample, state_hgrn, state_gla = f(x_prompt), f(x_sample), f(state_hgrn), f(state_gla)
    p_prompt, p_sample = f(p_prompt), f(p_sample)
    shared = dict(
        w_in=f(w_in[0]), w_branch=f(w_branch[0]), w_out=f(w_out[0]), w_gu=f(w_gu[0]), w_down=f(w_down[0]),
        w_ple=f(w_ple[0]), w_pg=f(w_pg[0]), ln1=f(ln1[0]), ln2=f(ln2[0]), ln3=f(ln3[0]), ln_f=f(ln_f),
        hg_lbT=f(np.asarray(hg_lb).reshape(2, 8, 128).transpose(2, 0, 1).reshape(128, 16)),
        hg_norm=f(hg_norm[0]), gla_norm=f(gla_norm[0]), w_gk=f(gla_w_gk[0]),
        b_gkT=f(np.asarray(gla_b_gk[0]).reshape(4, 128).T),
    )
    shared.update(_consts())
    zeros_pre = np.zeros((NTP, D), np.float32)
    in_maps = []
    for c in range(8):
        b, r = c // 2, c % 2
        xs = x_sample[c * NSQ:(c + 1) * NSQ].reshape(NTS, D)
        ps = p_sample[0, c * NSQ:(c + 1) * NSQ].reshape(NTS, 256)
        m = dict(shared)
        m["xcat"] = np.ascontiguousarray(np.concatenate([x_prompt[b, r * NTP:(r + 1) * NTP], xs], axis=0))
        m["xpre"] = np.ascontiguousarray(x_prompt[b, 0:NTP]) if r == 1 else zeros_pre
        m["pcat"] = np.ascontiguousarray(np.concatenate([p_prompt[0, b, r * NTP:(r + 1) * NTP], ps], axis=0))
        m["sth"] = np.ascontiguousarray(state_hgrn[0, c * NSQ:(c + 1) * NSQ])
        m["stg"] = np.ascontiguousarray(state_gla[0, c * NSQ:(c + 1) * NSQ])
        in_maps.append(m)
    return in_maps


def kernel(**inputs):
    in_maps = make_in_maps(**inputs)
    if "nc" not in _NC_CACHE:
        _NC_CACHE["nc"] = build_nc()
    nc = _NC_CACHE["nc"]
    res = run_bass_kernel_spmd(nc, in_maps, core_ids=list(range(8)))
    outs = res.results
    y_prompt = np.empty((4, 2048, D), np.float32)
    y_sample = np.empty((128, 4, D), np.float32)
    hp = np.empty((1, 4, 8, 128, 128), np.float32)
    gp = np.empty((1, 4, 4, 128, 256), np.float32)
    hs = np.empty((1, 128, 8, 128, 128), np.float32)
    gs = np.empty((1, 128, 4, 128, 256), np.float32)
    for c in range(8):
        b, r = c // 2, c % 2
        o = outs[c]
        y_prompt[b, r * NTP:(r + 1) * NTP] = o["ycat"][0:NTP]
        y_sample[c * NSQ:(c + 1) * NSQ] = o["ycat"][NTP:NT].reshape(NSQ, 4, D)
        if r == 1:
            hp[0, b] = o["hgrn_p"]
            gp[0, b] = o["gla_p"]
        hs[0, c * NSQ:(c + 1) * NSQ] = o["hgrn_s"]
        gs[0, c * NSQ:(c + 1) * NSQ] = o["gla_s"]
    return (y_prompt, y_sample, hp, gp, hs, gs)
```

```python
import contextlib
import numpy as np
import concourse.bass as bass
import concourse.mybir as mybir
from concourse.bass_utils import run_bass_kernel_spmd

F32 = mybir.dt.float32
BF16 = mybir.dt.bfloat16
AF = mybir.ActivationFunctionType
ALU = mybir.AluOpType

D = 2048
NTP = 1024
NSQ = 16
NTS = 64
NT = NTP + NTS
KC = 16
IN_COLS = 11280
DFF = 5632
EPS = 1e-6
C_HQ, C_HF, C_HI, C_HG = 0, 1024, 2048, 3072
C_GQ, C_GK, C_GV, C_GG, C_GLR, C_MG = 4096, 4608, 5120, 6144, 7168, 7184

ENG_NAMES = ["pe", "act", "dve", "pool", "sp"]
SEM_LIMIT = 30000


class Res:
    __slots__ = ("name", "w", "r")

    def __init__(self, name):
        self.name = name
        self.w = {}
        self.r = {}


class Prog:
    def __init__(self, same_engine_sync=("act", "dve", "pool")):
        self.streams = {e: [] for e in ENG_NAMES}
        self.nsem = 0
        self.cur_sem = {}
        self.cur_cnt = {}
        for e in ENG_NAMES:
            self.cur_sem[e] = self._new_sem()
            self.cur_cnt[e] = 0
        self.seen = {e: {} for e in ENG_NAMES}
        self.dma_sems = {}
        self.same_sync = set(same_engine_sync)
        self.all_dma_tokens = {}

    def _new_sem(self):
        self.nsem += 1
        return self.nsem - 1

    def _collect(self, eng, reads, writes):
        deps = {}
        for r in reads:
            for s, c in r.w.items():
                if deps.get(s, 0) < c:
                    deps[s] = c
        for w in writes:
            for s, c in w.w.items():
                if deps.get(s, 0) < c:
                    deps[s] = c
            for s, c in w.r.items():
                if deps.get(s, 0) < c:
                    deps[s] = c
        waits = []
        seen = self.seen[eng]
        for s, c in deps.items():
            if seen.get(s, 0) < c:
                waits.append((s, c))
                seen[s] = c
        return waits

    def _commit(self, tok, reads, writes):
        s, c = tok
        for r in reads:
            if r.r.get(s, 0) < c:
                r.r[s] = c
        for w in writes:
            w.w = {s: c}
            w.r = {}

    def op(self, eng, fn, reads=(), writes=()):
        waits = self._collect(eng, reads, writes)
        if self.cur_cnt[eng] >= SEM_LIMIT:
            self.cur_sem[eng] = self._new_sem()
            self.cur_cnt[eng] = 0
        self.cur_cnt[eng] += 1
        tok = (self.cur_sem[eng], self.cur_cnt[eng])
        if eng not in self.same_sync:
            self.seen[eng][tok[0]] = tok[1]
        self.streams[eng].append((fn, waits, tok[0], 1))
        self._commit(tok, reads, writes)
        return tok

    def dma(self, eng, fns, reads=(), writes=(), key="dma"):
        if not isinstance(fns, (list, tuple)):
            fns = [fns]
        waits = self._collect(eng, reads, writes)
        ent = self.dma_sems.get(key)
        if ent is None or ent[1] + 16 * len(fns) >= SEM_LIMIT:
            ent = [self._new_sem(), 0]
            self.dma_sems[key] = ent
        for i, fn in enumerate(fns):
            ent[1] += 16
            self.streams[eng].append((fn, waits if i == 0 else [], ent[0], 16))
        tok = (ent[0], ent[1])
        self._commit(tok, reads, writes)
        self.all_dma_tokens[tok[0]] = tok[1]
        return tok

    def barrier(self):
        toks = []
        for s, c in self.all_dma_tokens.items():
            toks.append((s, c))
        for e in ENG_NAMES:
            if self.cur_cnt[e] > 0:
                toks.append((self.cur_sem[e], self.cur_cnt[e]))
        for e in ENG_NAMES:
            waits = []
            for s, c in toks:
                if self.seen[e].get(s, 0) < c:
                    waits.append((s, c))
                    self.seen[e][s] = c
            if waits:
                self.streams[e].append((None, waits, None, 0))

    def emit(self, nc):
        sems = [nc.alloc_semaphore(f"s{i}") for i in range(self.nsem)]
        streams = self.streams

        def run(eng, name):
            for fn, waits, incsem, incval in streams[name]:
                for s, c in waits:
                    eng.wait_ge(sems[s], c)
                if fn is None:
                    continue
                ins = fn(eng)
                if incsem is not None:
                    ins.then_inc(sems[incsem], incval)

        with nc.Block() as block:
            @block.tensor
            def _(e):
                run(e, "pe")

            @block.scalar
            def _(e):
                run(e, "act")

            @block.vector
            def _(e):
                run(e, "dve")

            @block.gpsimd
            def _(e):
                run(e, "pool")

            @block.sync
            def _(e):
                run(e, "sp")


def build_nc(debug_stop=None):
    nc = bass.Bass("TRN2", target_bir_lowering=False)
    P = Prog()

    def din(name, shape):
        return nc.dram_tensor(name, list(shape), F32, kind="ExternalInput").ap()

    def dout(name, shape):
        return nc.dram_tensor(name, list(shape), F32, kind="ExternalOutput").ap()

    x_d = din("xcat", [NT, D])
    xpre_d = din("xpre", [NTP, D])
    p_d = din("pcat", [NT, 256])
    sth_d = din("sth", [NSQ, 8, 128, 128])
    stg_d = din("stg", [NSQ, 4, 128, 256])
    w_in_d = din("w_in", [D, IN_COLS])
    w_br_d = din("w_branch", [2, 1024, D])
    w_out_d = din("w_out", [D, D])
    w_gu_d = din("w_gu", [D, 2 * DFF])
    w_dn_d = din("w_down", [DFF, D])
    w_ple_d = din("w_ple", [256, D])
    w_pg_d = din("w_pg", [D, D])
    ln_d = {k: din(k, [D]) for k in ("ln1", "ln2", "ln3", "ln_f")}
    hglb_d = din("hg_lbT", [128, 16])
    hgn_d = din("hg_norm", [128])
    glan_d = din("gla_norm", [256])
    wgk_d = din("w_gk", [16, 512])
    bgk_d = din("b_gkT", [128, 4])
    ident_d = din("ident", [128, 128])
    maskT_d = din("maskT", [128, 128])
    bmaskT_d = din("bmaskT", [64, 64])
    qmask_d = din("qmask", [128, 16 * 64])
    rowmask_d = din("rowmask", [64, 16])

    y_d = dout("ycat", [NT, D])
    hp_d = dout("hgrn_p", [8, 128, 128])
    gp_d = dout("gla_p", [4, 128, 256])
    hs_d = dout("hgrn_s", [NSQ, 8, 128, 128])
    gs_d = dout("gla_s", [NSQ, 4, 128, 256])

    base = 16512
    top = int(nc.sbuf_top)
    OFF_C = base
    SZ_C = 16384
    OFF_H = OFF_C + SZ_C
    SZ_H = KC * NT * 2
    OFF_R = OFF_H + SZ_H
    SZ_R = SZ_H
    OFF_X = OFF_R + SZ_R
    SZ_X = SZ_H + 5 * 8192
    OFF_W = OFF_X + SZ_X
    SZ_W = 3 * 16384
    assert OFF_W + SZ_W <= top, (OFF_W + SZ_W, top)

    cnt = [0]

    def at(off, shape, dt):
        cnt[0] += 1
        return nc.alloc_sbuf_tensor_at(f"t{cnt[0]}", list(shape), dt, offset=off)

    class Carver:
        def __init__(self, off, size):
            self.off = off
            self.end = off + size

        def take(self, shape, dt, nbytes=None):
            esz = 4 if dt == F32 else 2
            n = 1
            for s in shape[1:]:
                n *= s
            nb = n * esz if nbytes is None else nbytes
            nb = (nb + 31) // 32 * 32
            t = at(self.off, shape, dt)
            self.off += nb
            assert self.off <= self.end, "carver overflow"
            return t

    cc = Carver(OFF_C, SZ_C)
    ident = cc.take([128, 128], BF16)
    maskT = cc.take([128, 128], F32)
    bmaskT = cc.take([128, 64], F32)
    qmask = cc.take([128, 16, 64], BF16)
    rowmask = cc.take([128, 16], F32)
    hg_gain = cc.take([128, 128], F32)
    gla_gain = cc.take([128, 256], F32)
    hglb = cc.take([128, 16], F32)
    lb = cc.take([128, 8], F32)
    oml = cc.take([128, 8], F32)
    noml = cc.take([128, 8], F32)
    nbgk = cc.take([128, 4], F32)
    wgk = cc.take([128, 512], BF16)
    one1 = cc.take([128, 8], F32)
    sc_a = cc.take([128, 32], F32)
    sc_d = cc.take([128, 24], F32)
    sc_r = cc.take([128, 24], F32)
    nrm_s = cc.take([128, 8], F32)
    PSTATE_OFF = cc.off
    pst_h = at(PSTATE_OFF, [128, 8, 128], F32)
    pst_g = at(PSTATE_OFF + 4096, [128, 4, 256], F32)
    gamma_c = at(PSTATE_OFF, [128, D], F32)
    SB16B_OFF = PSTATE_OFF + 8192
    assert SB16B_OFF + 1024 <= OFF_C + SZ_C

    r_const = Res("const")
    r_small = Res("small")
    r_nrm = Res("nrm")
    r_nrmAB = [Res("nrmA"), Res("nrmB")]
    r_pst = [Res(f"pst{i}") for i in range(12)]
    r_gamma = Res("gamma")

    hT = at(OFF_H, [128, KC, NT], BF16)
    r_hT = Res("hT")
    brT = at(OFF_R, [128, KC, NT], BF16)
    r_brT = Res("brT")
    xs_ = Carver(OFF_X, SZ_X)
    T = [xs_.take([128, NT], F32) for _ in range(4)]
    r_Tg = [[Res(f"T{i}g{g}") for g in range(3)] for i in range(4)]
    QE = [xs_.take([128, NT], BF16) for _ in range(2)]
    r_QE = [Res("QE0"), Res("QE1")]
    KE = xs_.take([128, NT], BF16)
    r_KE = Res("KE")
    QEF = at(OFF_X + 4 * NT * 4, [128, NT], F32)
    r_QEFg = [Res(f"QEFg{g}") for g in range(3)]
    BSETS = [(T[1], r_Tg[1], T[2], r_Tg[2]), (T[0], r_Tg[0], QEF, r_QEFg)]
    KOUT = xs_.take([128, NT], BF16)
    r_KOUT = Res("KOUT")
    KOTOK = xs_.take([128, 9, 128], BF16)
    r_KOTOK = Res("KOTOK")
    ATOK_OFF = xs_.off
    ATOK = xs_.take([128, 9, 256], BF16)
    r_ATOK = Res("ATOK")
    ATOKF = at(ATOK_OFF, [128, 2, 512], F32)
    VTOK = xs_.take([128, 9, 512], BF16)
    r_VTOK = Res("VTOK")
    GSG = xs_.take([128, 9, 512], BF16)
    r_GSG = Res("GSG")
    NS0, NSB = 3, 2
    S0 = [xs_.take([128, 2, 256], F32) for _ in range(NS0)]
    r_S0 = [Res(f"S0_{i}") for i in range(NS0)]
    SB16 = [xs_.take([128, 2, 256], BF16), at(SB16B_OFF, [128, 2, 256], BF16)]
    r_SB16 = [Res(f"SB16_{i}") for i in range(NSB)]
    s0cnt = [0]
    QBLK = xs_.take([128, 16, 64], BF16)
    r_QBLK = Res("QBLK")
    KBLK = xs_.take([128, 16, 128], BF16)
    r_KBLK = Res("KBLK")
    QINS = [xs_.take([128, 64], BF16) for _ in range(2)]
    r_QINS = [Res("QINS0"), Res("QINS1")]
    ATTM8 = xs_.take([128, 8, 128], BF16)
    r_ATTM8 = Res("ATTM8")
    ATTMS = xs_.take([128, 64], BF16)
    r_ATTMS = Res("ATTMS")
    SBP8 = xs_.take([128, 8, 256], BF16)
    r_SBP8 = Res("SBP8")
    JUNKV = xs_.take([128, 256], BF16)
    r_JUNKV = Res("JUNKV")
    SM = [xs_.take([128, 64], F32) for _ in range(3)]
    r_SM = [Res(f"SM{i}") for i in range(3)]
    SCR = [xs_.take([128, 40], F32) for _ in range(2)]
    r_SCR = [Res("SCR0"), Res("SCR1")]
    nrm4 = xs_.take([128, 8], F32)
    r_nrm2 = Res("nrm2")
    GLR = xs_.take([128, NT], BF16)
    r_GLR = Res("GLR")
    xn = Carver(OFF_X, SZ_X)
    XT = [xn.take([128, D], F32) for _ in range(2)]
    r_XT = [Res("XT0"), Res("XT1")]
    HB = xn.take([128, D], BF16)
    r_HB = Res("HB")
    HB2 = xn.take([128, D], BF16)
    r_HB2 = Res("HB2")
    JUNK = xn.take([128, D], BF16)
    r_JUNK = Res("JUNK")
    gamma_x = xn.take([128, D], F32)
    mergedT = at(OFF_X, [128, KC, NT], BF16)
    r_mergedT = Res("mergedT")
    actT = at(OFF_X, [128, 12, NT], BF16)
    r_actT = Res("actT")
    XR = []
    for i in range(9):
        if i < 4:
            XR.append(at(OFF_R + i * 8192, [128, D], F32))
        else:
            XR.append(at(OFF_X + SZ_H + (i - 4) * 8192, [128, D], F32))
    r_XR = [Res(f"XR{i}") for i in range(9)]
    WS = [at(OFF_W + i * 16384, [128, 8192], BF16) for i in range(3)]
    r_WS = [Res(f"W{i}") for i in range(3)]
    wcur = [0]
    ring = {"slots": WS, "res": r_WS}

    def set_ring(offsets, nbytes):
        ring["slots"] = [at(off, [128, nbytes // 2], BF16) for off in offsets]
        ring["res"] = [Res(f"Wr{i}") for i in range(len(offsets))]
        wcur[0] = 0

    PSB = [nc.alloc_psum_tensor(f"ps{i}", [128, 512], F32) for i in range(8)]
    r_PS = [Res(f"ps{i}") for i in range(8)]
    r_SL = {b: [r_PS[b]] * 4 for b in range(3, 8)}
    ring_state = {"proj": 0}

    def next_bank(banks):
        i = banks[ring_state["proj"] % len(banks)]
        ring_state["proj"] += 1
        return PSB[i], r_PS[i]

    PROJ_BANKS = [0, 1, 2]
    ALL_BANKS = [0, 1, 2, 3, 4, 5, 6, 7]

    def MM(out, lhsT, rhs, start, stop, rd, wr):
        return P.op("pe", lambda e: e.matmul(out, lhsT=lhsT, rhs=rhs, start=start, stop=stop), rd, wr)

    def TR(out, in_, idn, rd, wr):
        return P.op("pe", lambda e: e.transpose(out=out, in_=in_, identity=idn), rd, wr)

    def ACT(out, in_, func, rd, wr, bias=None, scale=None, accum=None):
        kw = {}
        if bias is not None:
            kw["bias"] = bias
        if scale is not None:
            kw["scale"] = scale
        if accum is not None:
            kw["accum_out"] = accum
        return P.op("act", lambda e: e.activation(out=out, in_=in_, func=func, **kw), rd, wr)

    def TS(eng, out, in0, s1, s2, op0, op1, rd, wr):
        if op1 is None:
            return P.op(eng, lambda e: e.tensor_scalar(out=out, in0=in0, scalar1=s1, scalar2=None, op0=op0), rd, wr)
        return P.op(eng, lambda e: e.tensor_scalar(out=out, in0=in0, scalar1=s1, scalar2=s2, op0=op0, op1=op1), rd, wr)

    def TT(eng, out, in0, in1, op, rd, wr):
        return P.op(eng, lambda e: e.tensor_tensor(out=out, in0=in0, in1=in1, op=op), rd, wr)

    def STT(out, in0, scalar, in1, op0, op1, rd, wr):
        return P.op("dve", lambda e: e.scalar_tensor_tensor(out=out, in0=in0, scalar=scalar, in1=in1, op0=op0, op1=op1), rd, wr)

    def RECIP(out, in_, rd, wr):
        return P.op("dve", lambda e: e.reciprocal(out=out, in_=in_), rd, wr)

    def CP(eng, out, in_, rd, wr):
        if eng == "act":
            return P.op("act", lambda e: e.copy(out=out, in_=in_), rd, wr)
        return P.op(eng, lambda e: e.tensor_copy(out=out, in_=in_), rd, wr)

    def MEMSET(eng, out, val, rd, wr):
        return P.op(eng, lambda e: e.memset(out, val), rd, wr)

    def DMA(eng, out, in_, rd, wr, key):
        return P.dma(eng, lambda e: e.dma_start(out=out, in_=in_), rd, wr, key=key)

    def load_w(src, kcn, ncols, key="w"):
        i = wcur[0] % len(ring["slots"])
        wcur[0] += 1
        WS_, r_WS_ = ring["slots"], ring["res"]
        view = WS_[i][:, 0:kcn * ncols].rearrange("p (k n) -> p k n", n=ncols)
        srcv = src.rearrange("(kc p) n -> p kc n", p=128)
        fns = []
        step = 4
        for k0 in range(0, kcn, step):
            k1 = min(kcn, k0 + step)
            fns.append(lambda e, k0=k0, k1=k1: e.dma_start(out=view[:, k0:k1, :], in_=srcv[:, k0:k1, :]))
        P.dma("pool", fns, [], [r_WS_[i]], key=f"w{i}")
        return view, r_WS_[i]

    P.dma("pool", [lambda e: e.dma_start(out=ident[:], in_=ident_d),
                   lambda e: e.dma_start(out=qmask[:], in_=qmask_d.rearrange("p (j t) -> p j t", t=64)),
                   lambda e: e.dma_start(out=wgk[0:16, :], in_=wgk_d)],
          [], [r_const], key="const")
    P.dma("sp", [lambda e: e.dma_start(out=maskT[:], in_=maskT_d),
                 lambda e: e.dma_start(out=bmaskT[0:64, :], in_=bmaskT_d),
                 lambda e: e.dma_start(out=rowmask[0:64, :], in_=rowmask_d),
                 lambda e: e.dma_start(out=hg_gain[:], in_=hgn_d.partition_broadcast(128)),
                 lambda e: e.dma_start(out=gla_gain[:], in_=glan_d.partition_broadcast(128)),
                 lambda e: e.dma_start(out=hglb[:], in_=hglb_d),
                 lambda e: e.dma_start(out=nbgk[:], in_=bgk_d),
                 ],
          [], [r_const], key="const2")
    MEMSET("dve", one1[:], 1.0, [], [r_const])
    TT("dve", lb[:], hglb[:, 8:16], hglb[:, 0:8], ALU.subtract, [r_const], [r_const])
    ACT(lb[:], lb[:], AF.Exp, [r_const], [r_const])
    TS("dve", lb[:], lb[:], 1.0, None, ALU.add, None, [r_const], [r_const])
    RECIP(lb[:], lb[:], [r_const], [r_const])
    TS("dve", oml[:], lb[:], -1.0, 1.0, ALU.mult, ALU.add, [r_const], [r_const])
    TS("dve", noml[:], lb[:], -1.0, None, ALU.add, None, [r_const], [r_const])
    TS("dve", nbgk[:], nbgk[:], -1.0, None, ALU.mult, None, [r_const], [r_const])
    for i in range(8):
        MEMSET("dve", pst_h[:, i, :], 0.0, [], [r_pst[i]])
    for i in range(4):
        MEMSET("dve", pst_g[:, i, :], 0.0, [], [r_pst[8 + i]])


    ones_b = lambda n: one1[:, 0:1].broadcast_to([128, n])

    def rstd_from_ss(ss, rstd, n, dim, rd_extra):
        TS("dve", rstd, ss, 1.0 / dim, EPS, ALU.mult, ALU.add, [r_nrm] + rd_extra, [r_nrm])
        ACT(rstd, rstd, AF.Ln, [r_nrm], [r_nrm])
        ACT(rstd, rstd, AF.Exp, [r_nrm], [r_nrm], scale=-0.5)

    ncnt = [0]

    def norm_stage1(src_tile, r_src, ntok, gamma, r_gam):
        k_ = ncnt[0] % 2
        ncnt[0] += 1
        hb, r_hb = (HB, r_HB) if k_ == 0 else (HB2, r_HB2)
        r_n = r_nrmAB[k_]
        ss = nrm_s[0:ntok, 4 * k_:4 * k_ + 1]
        rstd = nrm_s[0:ntok, 4 * k_ + 1:4 * k_ + 2]
        ACT(JUNK[0:ntok, :], src_tile[0:ntok, :], AF.Square, [r_src], [r_JUNK, r_n], accum=ss)
        TS("dve", rstd, ss, 1.0 / D, EPS, ALU.mult, ALU.add, [r_n], [r_n])
        ACT(rstd, rstd, AF.Ln, [r_n], [r_n])
        ACT(rstd, rstd, AF.Exp, [r_n], [r_n], scale=-0.5)
        STT(hb[0:ntok, :], src_tile[0:ntok, :], rstd, gamma[0:ntok, :], ALU.mult, ALU.mult,
            [r_src, r_n, r_gam], [r_hb])
        return hb, r_hb

    def norm_stage2(hb, r_hb, ntok, dstT, r_dst, col0):
        for half in range(2):
            pb, r_pb = next_bank(ALL_BANKS)
            pbv = pb[:, :].bitcast(BF16).rearrange("p (k n) -> p k n", n=128)
            for k in range(8):
                kc = half * 8 + k
                TR(pbv[:, k, 0:ntok], hb[0:ntok, kc * 128:(kc + 1) * 128], ident[0:ntok, 0:ntok],
                   [r_hb, r_const], [r_pb])
            eng = "act" if half == 0 else "dve"
            CP(eng, dstT[:, half * 8:(half + 1) * 8, col0:col0 + ntok], pbv[:, 0:8, 0:ntok], [r_pb], [r_dst])

    def norm_pipeline(tile_srcs, tiles, gamma, r_gam, dstT, r_dst):
        prev = None
        for ti, (c0, nt) in enumerate(tiles):
            src, r_src = tile_srcs(ti, c0, nt)
            hb, r_hb = norm_stage1(src, r_src, nt, gamma, r_gam)
            if prev is not None:
                norm_stage2(*prev)
            prev = (hb, r_hb, nt, dstT, r_dst, c0)
        norm_stage2(*prev)

    def norm_phase_from_dram(src_d, tiles, gamma, r_gam, dstT, r_dst):
        def srcs(ti, c0, nt):
            xt, r_xt = XT[ti % 2], r_XT[ti % 2]
            DMA("sp", xt[0:nt, :], src_d[c0:c0 + nt, :], [], [r_xt], key=f"xt{ti % 2}")
            return xt, r_xt
        norm_pipeline(srcs, tiles, gamma, r_gam, dstT, r_dst)

    TILES_MAIN = [(i * 128, 128) for i in range(8)] + [(1024, 64)]
    TILES_PRE = [(i * 128, 128) for i in range(8)]
    TG_MAIN = [(0, 384), (384, 768), (768, 1088)]
    TG_PRE = [(0, 384), (384, 768), (768, 1024)]
    TGI = {0: 0, 384: 1, 768: 2}

    def fproj(wv, r_w, kcn, coff, srcT, r_src, kc_off, tgs, evac, ncol=128, banks=PROJ_BANKS):
        for (t0, t1) in tgs:
            pb, r_pb = next_bank(banks)
            for kc in range(kcn):
                MM(pb[0:ncol, 0:t1 - t0], wv[:, kc, coff:coff + ncol], srcT[:, kc_off + kc, t0:t1],
                   kc == 0, kc == kcn - 1, [r_w, r_src], [r_pb])
            evac(pb, r_pb, t0, t1)

    def tproj(wv, r_w, kcn, ncols, srcT, r_src, tiles, evac, banks=PROJ_BANKS):
        for ti, (c0, nt) in enumerate(tiles):
            pb, r_pb = next_bank(banks)
            for kc in range(kcn):
                MM(pb[0:nt, 0:ncols], srcT[:, kc, c0:c0 + nt], wv[:, kc, :], kc == 0, kc == kcn - 1,
                   [r_w, r_src], [r_pb])
            evac(pb, r_pb, ti, nt)

    def sig_inplace(buf_ap, rd, wr):
        TS("dve", buf_ap, buf_ap, 1.0, None, ALU.add, None, rd, wr)
        RECIP(buf_ap, buf_ap, wr, wr)

    def act_sigmoid(dst, src_psum, rd_src, r_dst):
        ACT(dst, src_psum, AF.Exp, rd_src, r_dst, scale=-1.0)
        ACT(dst, dst, AF.Ln, r_dst, r_dst, bias=1.0)
        ACT(dst, dst, AF.Exp, r_dst, r_dst, scale=-1.0)

    def prep_steps(main, par, bset=0):
        sq, tmp = T[0], T[3]
        kk, r_kk, lf, r_lf = BSETS[bset]
        scr, r_scr = SCR[par], r_SCR[par]
        cum3 = tmp[:, 0:NTP].rearrange("p (t c) -> p t c", c=128)
        lf3 = lf[:, 0:NTP].rearrange("p (t c) -> p t c", c=128)
        cm, cl, cb, = sc_a[:, 0:8], sc_a[:, 8:16], sc_a[:, 16:24]
        lfs = lf[:, NTP:NT].rearrange("p (j t) -> p j t", t=4)
        cs, dls, es = SM[0], SM[1], SM[2]
        cs3 = cs[:, :].rearrange("p (j t) -> p j t", t=4)
        dl3 = dls[:, :].rearrange("p (j t) -> p j t", t=4)
        refb = cs3[:, :, 2:3].broadcast_to([128, 16, 4])
        lastb = cs3[:, :, 3:4].broadcast_to([128, 16, 4])

        def p0():
            P.op("dve", lambda e: e.tensor_tensor_scan(out=tmp[:, 0:NTP], data0=ones_b(NTP), data1=lf[:, 0:NTP],
                                                      initial=0.0, op0=ALU.mult, op1=ALU.add),
                 [*r_lf, r_const], [*r_Tg[3]])

        def p1():
            CP("dve", cm, cum3[:, :, 64], [*r_Tg[3]], [r_small])
            CP("dve", cl, cum3[:, :, 127], [*r_Tg[3]], [r_small])
            MEMSET("dve", sc_a[:, 16:17], 0.0, [], [r_small])
            CP("dve", sc_a[:, 17:24], sc_a[:, 8:15], [r_small], [r_small])
            TT("dve", sc_d[:, 0:8], cm, cb, ALU.subtract, [r_small], [r_small])
            TT("dve", sc_d[:, 8:16], cl, cb, ALU.subtract, [r_small], [r_small])
            TT("dve", sc_d[:, 16:24], cl, cm, ALU.subtract, [r_small], [r_small])
            if main:
                TT("dve", lf3, cum3, cm.unsqueeze(2).broadcast_to([128, 8, 128]), ALU.subtract,
                   [*r_Tg[3], r_small], [*r_lf])
            else:
                TT("dve", lf3, cl.unsqueeze(2).broadcast_to([128, 8, 128]), cum3, ALU.subtract,
                   [*r_Tg[3], r_small], [*r_lf])

        def p2():
            ACT(scr[:, 0:24], sc_d[:, :], AF.Exp, [r_small], [r_scr])
            ACT(tmp[:, 0:NTP], lf[:, 0:NTP], AF.Exp, [*r_lf], [*r_Tg[3]])

        def p3():
            TT("pool", QE[par][:, 0:NTP], sq[:, 0:NTP], tmp[:, 0:NTP], ALU.mult, [*r_Tg[0], *r_Tg[3]], [r_QE[par]])

        def p4():
            ACT(tmp[:, 0:NTP], lf[:, 0:NTP], AF.Exp, [*r_lf], [*r_Tg[3]], scale=-1.0)

        def p5():
            TT("pool", KE[:, 0:NTP], kk[:, 0:NTP], tmp[:, 0:NTP], ALU.mult, [*r_kk, *r_Tg[3]], [r_KE])
            TT("pool", KOUT[:, 0:NTP].rearrange("p (t c) -> p t c", c=128),
               KE[:, 0:NTP].rearrange("p (t c) -> p t c", c=128),
               scr[:, 16:24].unsqueeze(2).broadcast_to([128, 8, 128]), ALU.mult, [r_KE, r_scr], [r_KOUT])

        def p5_prefix():
            TT("pool", KOUT[:, 0:NTP], kk[:, 0:NTP], tmp[:, 0:NTP], ALU.mult, [*r_kk, *r_Tg[3]], [r_KOUT])

        def q0():
            CP("pool", cs3[:, :, 0], lfs[:, :, 0], [*r_lf], [r_SM[0]])
            for t in range(1, 4):
                TT("pool", cs3[:, :, t], cs3[:, :, t - 1], lfs[:, :, t], ALU.add, [*r_lf, r_SM[0]], [r_SM[0]])
            TT("pool", dl3, cs3, refb, ALU.subtract, [r_SM[0]], [r_SM[1]])

        def q1():
            ACT(es[:, :], dls[:, :], AF.Exp, [r_SM[1]], [r_SM[2]])

        def q2():
            TT("pool", QE[par][:, NTP:NT], sq[:, NTP:NT], es[:, :], ALU.mult, [*r_Tg[0], r_SM[2]], [r_QE[par]])

        def q3():
            ACT(es[:, :], dls[:, :], AF.Exp, [r_SM[1]], [r_SM[2]], scale=-1.0)

        def q4():
            TT("pool", KE[:, NTP:NT], kk[:, NTP:NT], es[:, :], ALU.mult, [*r_kk, r_SM[2]], [r_KE])

        def q5():
            ACT(es[:, :], cs[:, :], AF.Exp, [r_SM[0]], [r_SM[2]])

        def q6():
            TT("pool", QINS[par][:, :], sq[:, NTP:NT], es[:, :], ALU.mult, [*r_Tg[0], r_SM[2]], [r_QINS[par]])
            TT("pool", dl3, lastb, cs3, ALU.subtract, [r_SM[0]], [r_SM[1]])

        def q7():
            ACT(es[:, :], dls[:, :], AF.Exp, [r_SM[1]], [r_SM[2]])
            ACT(scr[:, 24:40], cs3[:, :, 3], AF.Exp, [r_SM[0]], [r_scr])

        def q8():
            TT("pool", KOUT[:, NTP:NT], kk[:, NTP:NT], es[:, :], ALU.mult, [*r_kk, r_SM[2]], [r_KOUT])

        def tr():
            pb, r_pb = next_bank(PROJ_BANKS)
            pbv = pb[:, :].bitcast(BF16).rearrange("p (k n) -> p k n", n=128)
            for t in range(8):
                TR(pbv[:, t, :], KOUT[:, t * 128:(t + 1) * 128], ident[:, :], [r_KOUT, r_const], [r_pb])
            CP("act" if main else "dve", KOTOK[:, 0:8, :], pbv[:, 0:8, :], [r_pb], [r_KOTOK])
            if main:
                pb, r_pb = next_bank(PROJ_BANKS)
                pbv2 = pb[:, :].bitcast(BF16)
                TR(pbv2[0:64, 0:128], KOUT[:, NTP:NT], ident[:, :], [r_KOUT, r_const], [r_pb])
                CP("act", KOTOK[0:64, 8, :], pbv2[0:64, 0:128], [r_pb], [r_KOTOK])

        def pE_prefix():
            ACT(lf[:, 0:NTP], tmp[:, 0:NTP], AF.Exp, [*r_Tg[3]], [*r_lf], scale=-1.0, bias=tmp[:, NTP - 1:NTP])

        def pK_prefix():
            TT("pool", KOUT[:, 0:NTP], kk[:, 0:NTP], lf[:, 0:NTP], ALU.mult, [*r_kk, *r_lf], [r_KOUT])

        if main:
            return [p0, q0, p1, q1, p2, q2, p3, q3, p4, q4, p5, q5, q6, q7, q8, tr]
        return [p0, pE_prefix, pK_prefix, tr]

    def scan(main, par, V, vcol0, r_state, S_ap, br_chunk0, state_src, state_dst, mid=None, fillers=()):
        fillers = list(fillers)
        scr, r_scr = SCR[par], r_SCR[par]
        nch = V // 128

        def kslot(g, tt):
            if V == 128:
                b = 4 + g
                return PSB[b][:, tt * 128:(tt + 1) * 128], [r_PS[b]]
            b = 4 + 2 * g + tt // 2
            q0 = (tt % 2) * 256
            return PSB[b][:, q0:q0 + 256], [r_PS[b]]

        def oslot(g, tt):
            if V == 128:
                b = 6 + g
                return PSB[b][:, tt * 128:(tt + 1) * 128], [r_PS[b]]
            return kslot(g, tt)

        def slot(base_bank, g, tt):
            return oslot(g, tt)

        def epilogue(g):
            for tt in range(4):
                po, r_po = oslot(g, tt)
                ACT(JUNKV[:, 0:V], po, AF.Square, r_po, [r_JUNKV, r_nrm2], accum=nrm4[:, tt:tt + 1])
            rstd_batch(nrm4[:, 0:4], nrm4[:, 4:8], V)
            for tt in range(4):
                t = g * 4 + tt
                po, r_po = oslot(g, tt)
                STT(ATOK[:, t, 0:V], po, nrm4[:, 4 + tt:5 + tt], GSG[:, t, vcol0:vcol0 + V], ALU.mult, ALU.mult,
                    r_po + [r_nrm2, r_GSG], [r_ATOK])

        if not main:
            pk, r_pk = PSB[4][:, 0:V], [r_PS[4]]
            for t in range(8):
                MM(pk, KOTOK[:, t, :], VTOK[:, t, vcol0:vcol0 + V], t == 0, t == 7, [r_KOTOK, r_VTOK], r_pk)
            CP("dve", S_ap, pk, r_pk, [r_state])
            if mid is not None:
                mid(None)
            for f in fillers:
                f()
            return
        for g in range(2):
            pa, r_pa = PSB[3], r_PS[3]
            if main:
                for tt in range(4):
                    t = g * 4 + tt
                    cs_ = slice(t * 128, (t + 1) * 128)
                    MM(pa[:, tt * 128:(tt + 1) * 128], KE[:, cs_], QE[par][:, cs_], True, True, [r_KE, r_QE[par]], [r_pa])
            for tt in range(4):
                t = g * 4 + tt
                pk, r_pk = kslot(g, tt)
                MM(pk, KOTOK[:, t, :], VTOK[:, t, vcol0:vcol0 + V], True, True, [r_KOTOK, r_VTOK], r_pk)
            if main:
                TT("dve", ATTM8[:, g * 4:(g + 1) * 4, :], pa[:, :].rearrange("p (t n) -> p t n", n=128),
                   maskT[:, :].unsqueeze(1).broadcast_to([128, 4, 128]), ALU.mult, [r_pa, r_const], [r_ATTM8])
            for tt in range(4):
                t = g * 4 + tt
                pk, r_pk = kslot(g, tt)
                if main:
                    TS("dve", SBP8[:, t, 0:V], S_ap, scr[:, t:t + 1], None, ALU.mult, None, [r_state, r_scr], [r_SBP8])
                STT(S_ap, S_ap, scr[:, 8 + t:9 + t], pk, ALU.mult, ALU.add, [r_state, r_scr] + r_pk, [r_state])
        if mid is not None:
            mid(0)
        if main:
            for g in range(2):
                for tt in range(4):
                    t = g * 4 + tt
                    cs_ = slice(t * 128, (t + 1) * 128)
                    po, r_po = oslot(g, tt)
                    MM(po, ATTM8[:, t, :], VTOK[:, t, vcol0:vcol0 + V], True, False, [r_ATTM8, r_VTOK], r_po)
                    MM(po, QE[par][:, cs_], SBP8[:, t, 0:V], False, True, [r_QE[par], r_SBP8], r_po)
            epilogue(0)
            epilogue(1)
            if mid is not None:
                mid(1)
            for _ in range(4):
                if fillers:
                    fillers.pop(0)()
        if not main:
            for f in fillers:
                f()
            return
        DMA("sp", state_dst[0], S_ap, [r_state], [], key="pstout")
        vt = VTOK[0:64, 8, vcol0:vcol0 + V]
        pa, r_pa = PSB[3], r_SL[3][0]
        MM(pa[0:64, 0:64], KE[:, NTP:NT], QE[par][:, NTP:NT], True, True, [r_KE, r_QE[par]], [r_pa])
        TT("dve", ATTMS[0:64, :], pa[0:64, 0:64], bmaskT[0:64, :], ALU.mult, [r_pa, r_const], [r_ATTMS])
        TT("pool", KBLK[0:64, :, :], KOTOK[0:64, 8, :].unsqueeze(1).broadcast_to([64, 16, 128]),
           rowmask[0:64, :].unsqueeze(2).broadcast_to([64, 16, 128]), ALU.mult, [r_KOTOK, r_const], [r_KBLK])
        TT("pool", QBLK[:, :, :], QINS[par][:, :].unsqueeze(1).broadcast_to([128, 16, 64]), qmask[:, :, :], ALU.mult,
           [r_QINS[par], r_const], [r_QBLK])
        po = PSB[6][0:64, 0:V]
        r_po = [r_SL[6][0], r_SL[6][1]] if V == 256 else [r_SL[6][0]]
        MM(po, ATTMS[0:64, :], vt, True, False, [r_ATTMS, r_VTOK], r_po)
        base = s0cnt[0]
        s0cnt[0] += 8

        def s0_load(pi):
            bi = (base + pi) % NS0
            DMA("sp", S0[bi][:, :, 0:V], state_src[2 * pi:2 * pi + 2].rearrange("j k v -> k j v"), [], [r_S0[bi]],
                key=f"s0in{bi}")
        for pi in range(NS0 - 1):
            s0_load(pi)
        for pi in range(8):
            bi = (base + pi) % NS0
            r_s0 = r_S0[bi]
            ci = pi % NSB
            sbp, r_sb = SB16[ci], r_SB16[ci]
            CP("act", sbp[:, :, 0:V], S0[bi][:, :, 0:V], [r_s0], [r_sb])
            bk = 4 + pi % 2
            r_pk = [r_PS[bk]]
            for jj in range(2):
                j = 2 * pi + jj
                MM(po, QBLK[:, j, :], sbp[:, jj, 0:V], False, (j == 15), [r_QBLK, r_sb], r_po)
            for jj in range(2):
                j = 2 * pi + jj
                MM(PSB[bk][:, jj * V:(jj + 1) * V], KBLK[0:64, j, :], vt, True, True, [r_KBLK, r_VTOK], r_pk)
            for jj in range(2):
                j = 2 * pi + jj
                s0 = S0[bi][:, jj, 0:V]
                STT(s0, s0, scr[:, 24 + j:25 + j], PSB[bk][:, jj * V:(jj + 1) * V], ALU.mult, ALU.add,
                    [r_s0, r_scr] + r_pk, [r_s0])
            if pi + NS0 - 1 < 8:
                s0_load(pi + NS0 - 1)
            DMA("sp", state_dst[1][2 * pi:2 * pi + 2].rearrange("j k v -> k j v"), S0[bi][:, :, 0:V], [r_s0], [],
                key=f"s0out{bi}")
            for _ in range(2):
                if fillers:
                    fillers.pop(0)()
        while fillers:
            fillers.pop(0)()
        ACT(JUNKV[0:64, 0:V], po, AF.Square, r_po, [r_JUNKV, r_nrm2], accum=nrm4[0:64, 0:1])
        rstd_batch(nrm4[0:64, 0:1], nrm4[0:64, 4:5], V)
        STT(ATOK[0:64, 8, 0:V], po, nrm4[0:64, 4:5], GSG[0:64, 8, vcol0:vcol0 + V], ALU.mult, ALU.mult,
            r_po + [r_nrm2, r_GSG], [r_ATOK])
        for c in range(nch):
            pb, r_pb = next_bank(PROJ_BANKS)
            pbv = pb[:, :].bitcast(BF16).rearrange("p (k n) -> p k n", n=128)
            for t in range(8):
                TR(pbv[:, t, :], ATOK[:, t, c * 128:(c + 1) * 128], ident[:, :], [r_ATOK, r_const], [r_pb])
            CP("act", brT[:, br_chunk0 + c, 0:NTP].rearrange("p (t n) -> p t n", n=128), pbv[:, 0:8, :], [r_pb], [r_brT])
            pb2, r_pb2 = next_bank(PROJ_BANKS)
            pbv2 = pb2[:, :].bitcast(BF16)
            TR(pbv2[:, 0:64], ATOK[0:64, 8, c * 128:(c + 1) * 128], ident[0:64, 0:64], [r_ATOK, r_const], [r_pb2])
            CP("act", brT[:, br_chunk0 + c, NTP:NT], pbv2[:, 0:64], [r_pb2], [r_brT])

    def rstd_batch(ss, rstd, dim):
        TS("dve", rstd, ss, 1.0 / dim, EPS, ALU.mult, ALU.add, [r_nrm2], [r_nrm2])
        ACT(rstd, rstd, AF.Ln, [r_nrm2], [r_nrm2])
        ACT(rstd, rstd, AF.Exp, [r_nrm2], [r_nrm2], scale=-0.5)

    def pipeline_prefix(jobs):
        n = len(jobs)
        jobs[0][0]()
        if n > 1:
            jobs[1][0]()
        for st in jobs[0][1]():
            st()
        for i in range(n):
            nx = jobs[i + 1][1]() if i + 1 < n else None
            if nx:
                nx[0]()
            jobs[i][2](None, [])
            if nx:
                nx[1]()
            if i + 2 < n:
                jobs[i + 2][0]()
            if nx:
                nx[2]()
                nx[3]()

    def pipeline(jobs, pre_done=False):
        n = len(jobs)
        if not pre_done:
            jobs[0][0]()
            for st in jobs[0][1]():
                st()
        for i in range(n):
            if i + 1 < n:
                jobs[i][2](jobs[i + 1][0], jobs[i + 1][1]())
            else:
                jobs[i][2](None, [])

    def gate_evac(gain_ap, nh, vh, fillers=None):
        def ev_g(pb, r_pb, ti, nt):
            e = ATOKF[0:nt, ti % 2, :]
            r_e = [r_ATOK]
            act_sigmoid(e, pb[0:nt, 0:512], [r_pb], r_e)
            TT("dve", e, e, pb[0:nt, 0:512], ALU.mult, [*r_e, r_pb], [*r_e])
            TT("pool", GSG[0:nt, ti, :].rearrange("p (h v) -> p h v", v=vh),
               e.rearrange("p (h v) -> p h v", v=vh),
               gain_ap[0:nt, :].unsqueeze(1).broadcast_to([nt, nh, vh]), ALU.mult,
               [*r_e, r_const], [r_GSG])
            if fillers:
                for _ in range(2):
                    if fillers:
                        fillers.pop(0)()
        return ev_g

    def ev_v(pb, r_pb, ti, nt):
        CP("act", VTOK[0:nt, ti, :], pb[0:nt, 0:512], [r_pb], [r_VTOK])

    hcount = [0]

    nxt_blk = {}

    def first_block(col0):
        if "blk" in nxt_blk:
            return nxt_blk.pop("blk")
        return load_w(w_in_d[:, col0:col0 + 512], KC, 512)

    def hgrn_half(main, hh, srcT, r_src, tiles, tgs, next_col=None):
        wv, r_w = first_block(C_HI + hh * 512)
        if main:
            wgt, r_wgt = load_w(w_in_d[:, C_HG + hh * 512:C_HG + (hh + 1) * 512], KC, 512)
        tproj(wv, r_w, KC, 512, srcT, r_src, tiles, ev_v)
        if main:
            wq, r_wq = load_w(w_in_d[:, C_HQ + hh * 512:C_HQ + (hh + 1) * 512], KC, 512)
        wf, r_wf = load_w(w_in_d[:, C_HF + hh * 512:C_HF + (hh + 1) * 512], KC, 512)

        def half_start(jobs):
            jobs[0][0]()
            fl = list(jobs[0][1]())
            tproj(wgt, r_wgt, KC, 512, srcT, r_src, tiles, gate_evac(hg_gain, 4, 128, fl))
            while fl:
                fl.pop(0)()
            if next_col is not None:
                nxt_blk["blk"] = load_w(w_in_d[:, next_col:next_col + 512], KC, 512)
        if not main and next_col is not None:
            nxt_blk["blk"] = load_w(w_in_d[:, next_col:next_col + 512], KC, 512)
        jobs = []
        for hl in range(4):
            h = hh * 4 + hl
            par = hcount[0] % 2
            hcount[0] += 1

            bset = 0 if main else hl % 2

            def A(part=None, hl=hl, h=h, bset=bset):
                if main and part in (None, 0):
                    def ev_q(pb, r_pb, t0, t1):
                        n = t1 - t0
                        g = TGI[t0]
                        e = T[0][:, t0:t1]
                        act_sigmoid(e, pb[:, 0:n], [r_pb], [r_Tg[0][g]])
                        STT(e, pb[:, 0:n], 128.0 ** -0.5, e, ALU.mult, ALU.mult, [r_pb, r_Tg[0][g]], [r_Tg[0][g]])
                    fproj(wq, r_wq, KC, hl * 128, srcT, r_src, 0, tgs, ev_q)
                if part == 0:
                    return

                kkb, r_kkb, lfb, r_lfb = BSETS[bset]

                def ev_f(pb, r_pb, t0, t1):
                    n = t1 - t0
                    g = TGI[t0]
                    e = kkb[:, t0:t1]
                    act_sigmoid(e, pb[:, 0:n], [r_pb], [r_kkb[g]])
                    ACT(lfb[:, t0:t1], e, AF.Ln, [r_kkb[g], r_const], [r_lfb[g]], scale=oml[:, h:h + 1], bias=lb[:, h:h + 1])
                    TS("dve", e, e, noml[:, h:h + 1], oml[:, h:h + 1], ALU.mult, ALU.add,
                       [r_kkb[g], r_const], [r_kkb[g]])
                fproj(wf, r_wf, KC, hl * 128, srcT, r_src, 0, tgs, ev_f)

            def B(par=par, bset=bset):
                return prep_steps(main, par, bset)

            def C(mid, fillers, hl=hl, h=h, par=par):
                scan(main, par, 128, hl * 128, r_pst[h], pst_h[:, h, :], h,
                     sth_d[:, h] if main else None, (hp_d[h], hs_d[:, h]) if main else None, mid, fillers)
            jobs.append((A, B, C))
        if main:
            half_start(jobs)
            pipeline(jobs, pre_done=True)
        else:
            pipeline_prefix(jobs)

    def gla_prep_glr(srcT, r_src, tgs):
        wv, r_w = load_w(w_in_d[:, C_GLR:C_GLR + 16], KC, 16)

        def ev(pb, r_pb, t0, t1):
            CP("act", GLR[0:16, t0:t1], pb[0:16, 0:t1 - t0], [r_pb], [r_GLR])
        fproj(wv, r_w, KC, 0, srcT, r_src, 0, tgs, ev, ncol=16)

    def gla_half(main, hh, srcT, r_src, tiles, tgs, next_col=None):
        wv, r_w = first_block(C_GV + hh * 512)
        if main:
            wgt, r_wgt = load_w(w_in_d[:, C_GG + hh * 512:C_GG + (hh + 1) * 512], KC, 512)
        tproj(wv, r_w, KC, 512, srcT, r_src, tiles, ev_v)
        if main:
            wq, r_wq = load_w(w_in_d[:, C_GQ + hh * 256:C_GQ + (hh + 1) * 256], KC, 256)
        wk, r_wk = load_w(w_in_d[:, C_GK + hh * 256:C_GK + (hh + 1) * 256], KC, 256)

        def half_start(jobs):
            jobs[0][0]()
            fl = list(jobs[0][1]())
            tproj(wgt, r_wgt, KC, 512, srcT, r_src, tiles, gate_evac(gla_gain, 2, 256, fl))
            while fl:
                fl.pop(0)()
            if next_col is not None:
                nxt_blk["blk"] = load_w(w_in_d[:, next_col:next_col + 512], KC, 512)
        if not main and next_col is not None:
            nxt_blk["blk"] = load_w(w_in_d[:, next_col:next_col + 512], KC, 512)
        jobs = []
        for hl in range(2):
            h = hh * 2 + hl
            par = hcount[0] % 2
            hcount[0] += 1

            bset = 0 if main else hl % 2

            def A(part=None, hl=hl, h=h, bset=bset):
                kkb, r_kkb, lfb, r_lfb = BSETS[bset]
                if main and part in (None, 0):
                    def ev_q(pb, r_pb, t0, t1):
                        P.op("act", lambda e, o=T[0][:, t0:t1], i=pb[:, 0:t1 - t0]: e.mul(out=o, in_=i, mul=128.0 ** -0.5),
                             [r_pb], [r_Tg[0][TGI[t0]]])
                    fproj(wq, r_wq, KC, hl * 128, srcT, r_src, 0, tgs, ev_q)
                if part == 0:
                    return

                def ev_k(pb, r_pb, t0, t1):
                    CP("act", kkb[:, t0:t1], pb[:, 0:t1 - t0], [r_pb], [r_kkb[TGI[t0]]])
                fproj(wk, r_wk, KC, hl * 128, srcT, r_src, 0, tgs, ev_k)
                for (t0, t1) in tgs:
                    pb, r_pb = next_bank(PROJ_BANKS)
                    n = t1 - t0
                    MM(pb[:, 0:n], wgk[0:16, h * 128:(h + 1) * 128], GLR[0:16, t0:t1], True, True, [r_const, r_GLR], [r_pb])
                    g = TGI[t0]
                    e = lfb[:, t0:t1]
                    ACT(e, pb[:, 0:n], AF.Exp, [r_pb, r_const], [r_lfb[g]], scale=-1.0, bias=nbgk[:, h:h + 1])
                    ACT(e, e, AF.Ln, [r_lfb[g]], [r_lfb[g]], bias=1.0)
                    TS("dve", e, e, -1.0 / 16.0, None, ALU.mult, None, [r_lfb[g]], [r_lfb[g]])

            def B(par=par, bset=bset):
                return prep_steps(main, par, bset)

            def C(mid, fillers, hl=hl, h=h, par=par):
                scan(main, par, 256, hl * 256, r_pst[8 + h], pst_g[:, h, :], 8 + 2 * h,
                     stg_d[:, h] if main else None, (gp_d[h], gs_d[:, h]) if main else None, mid, fillers)
            jobs.append((A, B, C))
        if main:
            half_start(jobs)
            pipeline(jobs, pre_done=True)
        else:
            pipeline_prefix(jobs)

    def dbg_finish(kind):
        if kind == "xr":
            for ti, (c0, nt) in enumerate(TILES_MAIN):
                DMA("sp", y_d[c0:c0 + nt, :], XR[ti][0:nt, :], [r_XR[ti]], [], key="yout")
        else:
            dbg_d = nc.dram_tensor("dbg", [128, KC, NT], F32, kind="ExternalOutput").ap()
            src = {"brT": brT, "mergedT": mergedT, "hT": hT}[kind]
            P.dma("pool", [lambda e, k=k: e.dma_start(out=dbg_d[:, k * 4:(k + 1) * 4, :], in_=src[:, k * 4:(k + 1) * 4, :]) for k in range(4)],
                  [], [], key="dbgout")
        P.barrier()
        P.emit(nc)
        return nc

    DMA("sp", gamma_x[:], ln_d["ln1"].partition_broadcast(128), [], [r_gamma], key="gamma")
    nxt_blk["blk"] = load_w(w_in_d[:, C_HI:C_HI + 512], KC, 512)
    norm_phase_from_dram(xpre_d, TILES_PRE, gamma_x, r_gamma, hT, r_hT)
    P.barrier()
    hgrn_half(False, 0, hT, r_hT, TILES_PRE, TG_PRE, next_col=C_HI + 512)
    hgrn_half(False, 1, hT, r_hT, TILES_PRE, TG_PRE, next_col=C_GV)
    gla_prep_glr(hT, r_hT, TG_PRE)
    gla_half(False, 0, hT, r_hT, TILES_PRE, TG_PRE, next_col=C_GV + 512)
    gla_half(False, 1, hT, r_hT, TILES_PRE, TG_PRE)
    P.barrier()
    DMA("sp", gamma_x[:], ln_d["ln1"].partition_broadcast(128), [], [r_gamma], key="gamma")
    nxt_blk["blk"] = load_w(w_in_d[:, C_HI:C_HI + 512], KC, 512)
    norm_phase_from_dram(x_d, TILES_MAIN, gamma_x, r_gamma, hT, r_hT)
    P.barrier()
    hgrn_half(True, 0, hT, r_hT, TILES_MAIN, TG_MAIN, next_col=C_HI + 512)
    hgrn_half(True, 1, hT, r_hT, TILES_MAIN, TG_MAIN, next_col=C_GV)
    gla_prep_glr(hT, r_hT, TG_MAIN)
    gla_half(True, 0, hT, r_hT, TILES_MAIN, TG_MAIN, next_col=C_GV + 512)
    gla_half(True, 1, hT, r_hT, TILES_MAIN, TG_MAIN)
    P.barrier()
    if debug_stop == "B":
        return dbg_finish("brT")
    set_ring([OFF_W, OFF_W + 16384, OFF_W + 32768, OFF_X + SZ_H + 8192, OFF_X + SZ_H + 24576], 16384)
    G0 = at(OFF_X + SZ_H, [128, 384], F32)
    G1 = at(OFF_X + SZ_H + 1536, [128, 384], F32)
    M0 = at(OFF_X + SZ_H + 3072, [128, 384], F32)
    r_G0, r_G1, r_M0 = Res("G0"), Res("G1"), Res("M0")
    for q in range(4):
        wg0, r_wg0 = load_w(w_in_d[:, C_MG + q * 512:C_MG + (q + 1) * 512], KC, 512)
        wg1, r_wg1 = load_w(w_in_d[:, C_MG + 2048 + q * 512:C_MG + 2048 + (q + 1) * 512], KC, 512)
        i = wcur[0] % len(ring["slots"])
        wcur[0] += 1
        wbv = ring["slots"][i][:, 0:16 * 512].rearrange("p (k n) -> p k n", n=512)
        r_wb = ring["res"][i]
        fns = []
        for n in range(2):
            for k0 in (0, 4):
                fns.append(lambda e, n=n, k0=k0, q=q, wbv=wbv: e.dma_start(
                    out=wbv[:, n * 8 + k0:n * 8 + k0 + 4, :],
                    in_=w_br_d[n, :, q * 512:(q + 1) * 512].rearrange("(kc p) n -> p kc n", p=128)[:, k0:k0 + 4, :]))
        P.dma("pool", fns, [], [r_wb], key=f"w{i}")
        for j in range(4):
            dc = q * 4 + j
            for (t0, t1) in TG_MAIN:
                n = t1 - t0
                pg0, r_pg0 = next_bank(ALL_BANKS)
                for kc in range(KC):
                    MM(pg0[:, 0:n], wg0[:, kc, j * 128:(j + 1) * 128], hT[:, kc, t0:t1], kc == 0, kc == KC - 1, [r_wg0, r_hT], [r_pg0])
                pg1, r_pg1 = next_bank(ALL_BANKS)
                for kc in range(KC):
                    MM(pg1[:, 0:n], wg1[:, kc, j * 128:(j + 1) * 128], hT[:, kc, t0:t1], kc == 0, kc == KC - 1, [r_wg1, r_hT], [r_pg1])
                pu0, r_pu0 = next_bank(ALL_BANKS)
                for kc in range(8):
                    MM(pu0[:, 0:n], wbv[:, kc, j * 128:(j + 1) * 128], brT[:, kc, t0:t1], kc == 0, kc == 7, [r_wb, r_brT], [r_pu0])
                pu1, r_pu1 = next_bank(ALL_BANKS)
                for kc in range(8):
                    MM(pu1[:, 0:n], wbv[:, 8 + kc, j * 128:(j + 1) * 128], brT[:, 8 + kc, t0:t1], kc == 0, kc == 7, [r_wb, r_brT], [r_pu1])
                ACT(G0[:, 0:n], pg0[:, 0:n], AF.Exp, [r_pg0], [r_G0], scale=-1.0)
                sig_inplace(G0[:, 0:n], [r_G0], [r_G0])
                ACT(G1[:, 0:n], pg1[:, 0:n], AF.Exp, [r_pg1], [r_G1], scale=-1.0)
                sig_inplace(G1[:, 0:n], [r_G1], [r_G1])
                TT("dve", M0[:, 0:n], G0[:, 0:n], pu0[:, 0:n], ALU.mult, [r_G0, r_pu0], [r_M0])
                TT("dve", G1[:, 0:n], G1[:, 0:n], pu1[:, 0:n], ALU.mult, [r_G1, r_pu1], [r_G1])
                TT("dve", mergedT[:, dc, t0:t1], M0[:, 0:n], G1[:, 0:n], ALU.add, [r_M0, r_G1], [r_mergedT])
    P.barrier()
    if debug_stop == "B6":
        return dbg_finish("mergedT")
    set_ring([OFF_W, OFF_W + 16384, OFF_W + 32768], 16384)
    wout = [load_w(w_out_d[:, 0:512], KC, 512)]
    for ti, (c0, nt) in enumerate(TILES_MAIN):
        DMA("sp", XR[ti][0:nt, :], x_d[c0:c0 + nt, :], [wout[0][1]] if ti > 0 else [], [r_XR[ti]], key="xr")
    for cb in range(4):
        wv, r_w = wout[cb] if cb < len(wout) else load_w(w_out_d[:, cb * 512:(cb + 1) * 512], KC, 512)

        def ev_o(pb, r_pb, ti, nt, cb=cb):
            xv = XR[ti][0:nt, cb * 512:(cb + 1) * 512]
            TT("dve", xv, xv, pb[0:nt, 0:512], ALU.add, [r_XR[ti], r_pb], [r_XR[ti]])
        tproj(wv, r_w, KC, 512, mergedT, r_mergedT, TILES_MAIN, ev_o, banks=ALL_BANKS)
    P.barrier()
    if debug_stop == "C":
        return dbg_finish("xr")
    HB = at(OFF_X, [128, D], BF16)
    JUNK = at(OFF_X + 4096, [128, D], BF16)
    HB2 = at(OFF_X + 8192, [128, D], BF16)
    r_HB, r_JUNK, r_HB2 = Res("HBb"), Res("JUNKb"), Res("HB2b")

    def norm_resident(lnname, dstT, r_dst):
        DMA("sp", gamma_c[:], ln_d[lnname].partition_broadcast(128), [], [r_gamma], key="gamma")
        norm_pipeline(lambda ti, c0, nt: (XR[ti], r_XR[ti]), TILES_MAIN, gamma_c, r_gamma, dstT, r_dst)

    set_ring([OFF_W, OFF_W + 12288, OFF_W + 24576, OFF_W + 36864], 12288)
    ffn_pre = [load_w(w_gu_d[:, 0:384], KC, 384), load_w(w_gu_d[:, DFF:DFF + 384], KC, 384)]
    norm_resident("ln2", hT, r_hT)
    P.barrier()
    if debug_stop == "N2":
        return dbg_finish("hT")
    E0 = at(OFF_X + 26112, [128, 384], F32)
    E1 = at(OFF_X + 26112 + 1536, [128, 384], F32)
    r_E = [Res("E0"), Res("E1")]
    Eb = [E0, E1]
    ecnt = [0]
    fc0 = 0
    for gs in (12, 12, 12, 8):
        blocks = [3, 3, 3, 3] if gs == 12 else [3, 3, 2]
        c0_ = 0
        for bs in blocks:
            cg = (fc0 + c0_) * 128
            if ffn_pre:
                (wg, r_wg), (wu, r_wu) = ffn_pre
                ffn_pre = None
            else:
                wg, r_wg = load_w(w_gu_d[:, cg:cg + bs * 128], KC, bs * 128)
                wu, r_wu = load_w(w_gu_d[:, DFF + cg:DFF + cg + bs * 128], KC, bs * 128)
            for j in range(bs):
                c = c0_ + j
                for (t0, t1) in TG_MAIN:
                    n = t1 - t0
                    pg, r_pg = next_bank(ALL_BANKS)
                    for kc in range(KC):
                        MM(pg[:, 0:n], wg[:, kc, j * 128:(j + 1) * 128], hT[:, kc, t0:t1], kc == 0, kc == KC - 1, [r_wg, r_hT], [r_pg])
                    pu, r_pu = next_bank(ALL_BANKS)
                    for kc in range(KC):
                        MM(pu[:, 0:n], wu[:, kc, j * 128:(j + 1) * 128], hT[:, kc, t0:t1], kc == 0, kc == KC - 1, [r_wu, r_hT], [r_pu])
                    e, r_e = Eb[ecnt[0] % 2], r_E[ecnt[0] % 2]
                    ecnt[0] += 1
                    ACT(e[:, 0:n], pg[:, 0:n], AF.Exp, [r_pg], [r_e], scale=-1.0)
                    sig_inplace(e[:, 0:n], [r_e], [r_e])
                    TT("dve", e[:, 0:n], e[:, 0:n], pg[:, 0:n], ALU.mult, [r_e, r_pg], [r_e])
                    TT("dve", actT[:, c, t0:t1], e[:, 0:n], pu[:, 0:n], ALU.mult, [r_e, r_pu], [r_actT])
            c0_ += bs
        for cb in range(4):
            wd, r_wd = load_w(w_dn_d[fc0 * 128:(fc0 + gs) * 128, cb * 512:(cb + 1) * 512], gs, 512)

            def ev_d(pb, r_pb, ti, nt, cb=cb):
                xv = XR[ti][0:nt, cb * 512:(cb + 1) * 512]
                TT("dve", xv, xv, pb[0:nt, 0:512], ALU.add, [r_XR[ti], r_pb], [r_XR[ti]])
            tproj(wd, r_wd, gs, 512, actT, r_actT, TILES_MAIN, ev_d, banks=ALL_BANKS)
        fc0 += gs
    P.barrier()
    if debug_stop == "D":
        return dbg_finish("xr")
    set_ring([OFF_W, OFF_W + 16384, OFF_W + 32768], 16384)
    norm_resident("ln3", hT, r_hT)
    PTOK = at(OFF_X + 12288, [128, 9, 256], BF16)
    r_PTOK = Res("PTOK")
    pT = at(OFF_X + 12288 + 4608, [128, 2, NT], BF16)
    r_pT = Res("pT")
    WPLE = at(OFF_X + 12288 + 4608 + 4352, [128, 2, D], BF16)
    r_WPLE = Res("WPLE")
    SGs = [at(OFF_X + 12288 + 4608 + 4352 + 8192 + i * 2048, [128, 512], F32) for i in range(2)]
    r_SGs = [Res(f"SG{i}") for i in range(2)]
    sgc = [0]
    YB = [at(OFF_X + 8192 + i * 8192, [128, D], F32) for i in range(3)]
    r_YB = [Res(f"YB{i}") for i in range(3)]
    assert 12288 + 4608 + 4352 + 8192 + 2 * 2048 <= SZ_H
    P.dma("pool", [lambda e: e.dma_start(out=PTOK[:, 0:8, :], in_=p_d[0:NTP, :].rearrange("(t p) c -> p t c", p=128)),
                   lambda e: e.dma_start(out=PTOK[0:64, 8, :], in_=p_d[NTP:NT, :]),
                   lambda e: e.dma_start(out=WPLE[:], in_=w_ple_d.rearrange("(kc p) n -> p kc n", p=128))],
          [], [r_PTOK, r_WPLE], key="ptok")
    for kc in range(2):
        pb, r_pb = next_bank(ALL_BANKS)
        pbv = pb[:, :].bitcast(BF16).rearrange("p (k n) -> p k n", n=128)
        for t in range(8):
            TR(pbv[:, t, :], PTOK[:, t, kc * 128:(kc + 1) * 128], ident[:, :], [r_PTOK, r_const], [r_pb])
        CP("act", pT[:, kc, 0:NTP].rearrange("p (t n) -> p t n", n=128), pbv[:, 0:8, :], [r_pb], [r_pT])
        pb, r_pb = next_bank(ALL_BANKS)
        pbv2 = pb[:, :].bitcast(BF16)
        TR(pbv2[:, 0:64], PTOK[0:64, 8, kc * 128:(kc + 1) * 128], ident[0:64, 0:64], [r_PTOK, r_const], [r_pb])
        CP("act", pT[:, kc, NTP:NT], pbv2[:, 0:64], [r_pb], [r_pT])
    wpg = [load_w(w_pg_d[:, cb * 512:(cb + 1) * 512], KC, 512) for cb in range(3)]
    for cb in range(4):
        if cb == 1:
            wpg.append(load_w(w_pg_d[:, 3 * 512:4 * 512], KC, 512))
        wv, r_w = wpg[cb]
        for ti, (c0, nt) in enumerate(TILES_MAIN):
            pg, r_pg = next_bank(ALL_BANKS)
            for kc in range(KC):
                MM(pg[0:nt, 0:512], hT[:, kc, c0:c0 + nt], wv[:, kc, :], kc == 0, kc == KC - 1, [r_w, r_hT], [r_pg])
            pp, r_pp = next_bank(ALL_BANKS)
            for kc in range(2):
                MM(pp[0:nt, 0:512], pT[:, kc, c0:c0 + nt], WPLE[:, kc, cb * 512:(cb + 1) * 512], kc == 0, kc == 1, [r_WPLE, r_pT], [r_pp])
            SG, r_SG = SGs[sgc[0] % 2], r_SGs[sgc[0] % 2]
            sgc[0] += 1
            ACT(SG[0:nt, :], pg[0:nt, 0:512], AF.Exp, [r_pg], [r_SG], scale=-1.0)
            ACT(SG[0:nt, :], SG[0:nt, :], AF.Ln, [r_SG], [r_SG], bias=1.0)
            ACT(SG[0:nt, :], SG[0:nt, :], AF.Exp, [r_SG], [r_SG], scale=-1.0)
            TT("dve", SG[0:nt, :], SG[0:nt, :], pp[0:nt, 0:512], ALU.mult, [r_SG, r_pp], [r_SG])
            xv = XR[ti][0:nt, cb * 512:(cb + 1) * 512]
            TT("pool", xv, xv, SG[0:nt, :], ALU.add, [r_XR[ti], r_SG], [r_XR[ti]])
    P.barrier()
    if debug_stop == "E":
        return dbg_finish("xr")
    DMA("sp", gamma_c[:], ln_d["ln_f"].partition_broadcast(128), [r_gamma], [r_gamma], key="gamma")
    for ti, (c0, nt) in enumerate(TILES_MAIN):
        k_ = ti % 2
        r_n = r_nrmAB[k_]
        ss = nrm_s[0:nt, 4 * k_:4 * k_ + 1]
        rstd = nrm_s[0:nt, 4 * k_ + 1:4 * k_ + 2]
        ACT(JUNK[0:nt, :], XR[ti][0:nt, :], AF.Square, [r_XR[ti]], [r_JUNK, r_n], accum=ss)
        TS("dve", rstd, ss, 1.0 / D, EPS, ALU.mult, ALU.add, [r_n], [r_n])
        ACT(rstd, rstd, AF.Ln, [r_n], [r_n])
        ACT(rstd, rstd, AF.Exp, [r_n], [r_n], scale=-0.5)
        yb, r_yb = YB[ti % 3], r_YB[ti % 3]
        STT(yb[0:nt, :], XR[ti][0:nt, :], rstd, gamma_c[0:nt, :], ALU.mult, ALU.mult, [r_XR[ti], r_n, r_gamma], [r_yb])
        DMA("sp", y_d[c0:c0 + nt, :], yb[0:nt, :], [r_yb], [], key="yout")
    P.barrier()
    P.emit(nc)
    return nc


_NC_CACHE = {}


def _consts():
    ident = np.eye(128, dtype=np.float32)
    s = np.arange(128)
    maskT = (s[:, None] <= s[None, :]).astype(np.float32)
    s4 = np.arange(64)
    same = (s4[:, None] // 4) == (s4[None, :] // 4)
    bmaskT = (same & (s4[:, None] <= s4[None, :])).astype(np.float32)
    qm = ((np.arange(64)[None, :] // 4) == np.arange(16)[:, None]).astype(np.float32)
    qmask = np.broadcast_to(qm.reshape(1, 16 * 64), (128, 16 * 64)).copy()
    rowmask = ((np.arange(64)[:, None] // 4) == np.arange(16)[None, :]).astype(np.float32)
    return dict(ident=ident, maskT=maskT, bmaskT=bmaskT, qmask=qmask, rowmask=rowmask)


def make_in_maps(x_prompt, x_sample, state_hgrn, state_gla, p_prompt, p_sample, hg_lb, ln1, w_in,
                 hg_norm, gla_w_gk, gla_b_gk, gla_norm, w_branch, w_out, ln2, w_gu, w_down, ln3,
                 w_ple, w_pg, ln_f):
    f = lambda a: np.ascontiguousarray(np.asarray(a, dtype=np.float32))
    x_prompt, x_sample, state_hgrn, state_gla = f(x_prompt), f(x_sample), f(state_hgrn), f(state_gla)
    p_prompt, p_sample = f(p_prompt), f(p_sample)
    shared = dict(
        w_in=f(w_in[0]), w_branch=f(w_branch[0]), w_out=f(w_out[0]), w_gu=f(w_gu[0]), w_down=f(w_down[0]),
        w_ple=f(w_ple[0]), w_pg=f(w_pg[0]), ln1=f(ln1[0]), ln2=f(ln2[0]), ln3=f(ln3[0]), ln_f=f(ln_f),
        hg_lbT=f(np.asarray(hg_lb).reshape(2, 8, 128).transpose(2, 0, 1).reshape(128, 16)),
        hg_norm=f(hg_norm[0]), gla_norm=f(gla_norm[0]), w_gk=f(gla_w_gk[0]),
        b_gkT=f(np.asarray(gla_b_gk[0]).reshape(4, 128).T),
    )
    shared.update(_consts())
    zeros_pre = np.zeros((NTP, D), np.float32)
    in_maps = []
    for c in range(8):
        b, r = c // 2, c % 2
        xs = x_sample[c * NSQ:(c + 1) * NSQ].reshape(NTS, D)
        ps = p_sample[0, c * NSQ:(c + 1) * NSQ].reshape(NTS, 256)
        m = dict(shared)
        m["xcat"] = np.ascontiguousarray(np.concatenate([x_prompt[b, r * NTP:(r + 1) * NTP], xs], axis=0))
        m["xpre"] = np.ascontiguousarray(x_prompt[b, 0:NTP]) if r == 1 else zeros_pre
        m["pcat"] = np.ascontiguousarray(np.concatenate([p_prompt[0, b, r * NTP:(r + 1) * NTP], ps], axis=0))
        m["sth"] = np.ascontiguousarray(state_hgrn[0, c * NSQ:(c + 1) * NSQ])
        m["stg"] = np.ascontiguousarray(state_gla[0, c * NSQ:(c + 1) * NSQ])
        in_maps.append(m)
    return in_maps


def kernel(**inputs):
    in_maps = make_in_maps(**inputs)
    if "nc" not in _NC_CACHE:
        _NC_CACHE["nc"] = build_nc()
    nc = _NC_CACHE["nc"]
    res = run_bass_kernel_spmd(nc, in_maps, core_ids=list(range(8)))
    outs = res.results
    y_prompt = np.empty((4, 2048, D), np.float32)
    y_sample = np.empty((128, 4, D), np.float32)
    hp = np.empty((1, 4, 8, 128, 128), np.float32)
    gp = np.empty((1, 4, 4, 128, 256), np.float32)
    hs = np.empty((1, 128, 8, 128, 128), np.float32)
    gs = np.empty((1, 128, 4, 128, 256), np.float32)
    for c in range(8):
        b, r = c // 2, c % 2
        o = outs[c]
        y_prompt[b, r * NTP:(r + 1) * NTP] = o["ycat"][0:NTP]
        y_sample[c * NSQ:(c + 1) * NSQ] = o["ycat"][NTP:NT].reshape(NSQ, 4, D)
        if r == 1:
            hp[0, b] = o["hgrn_p"]
            gp[0, b] = o["gla_p"]
        hs[0, c * NSQ:(c + 1) * NSQ] = o["hgrn_s"]
        gs[0, c * NSQ:(c + 1) * NSQ] = o["gla_s"]
    return (y_prompt, y_sample, hp, gp, hs, gs)
```
